# Optimizing a Trainium2 kernel written in Bass

```python
import math
import jax, jax.numpy as jnp
from jax import lax
import numpy as np

D_MODEL = 2048
BATCH = 4
SEQ = 2048
DEPTH = 4

CHUNK = 64
D_MIX = D_MODEL
D_S5 = D_MIX // 2
D_CONV = D_MIX - D_S5
S5_P = 16
S5_G = D_S5 // S5_P
S5_N = 64
CONV_HEADS = 8
CONV_W = 3
D_IN = D_S5 + 3 * D_CONV
D_FF = 5632
LN_EPS = 1e-5
RMS_EPS = 1e-6
DEEPNORM_ALPHA = (2.0 * DEPTH) ** 0.25
DEEPNORM_BETA = (8.0 * DEPTH) ** -0.25

kernel_name = "hybrid_s5_shortconv_macaron_deepnorm"


def layer_norm(x, g, b):
    xf = x.astype(jnp.float32)
    mu = jnp.mean(xf, axis=-1, keepdims=True)
    xc = xf - mu
    var = jnp.mean(xc * xc, axis=-1, keepdims=True)
    y = xc * lax.rsqrt(var + LN_EPS)
    return (y * g.astype(jnp.float32) + b.astype(jnp.float32)).astype(x.dtype)


def rms_norm(x, g):
    xf = x.astype(jnp.float32)
    y = xf * lax.rsqrt(jnp.mean(xf * xf, axis=-1, keepdims=True) + RMS_EPS)
    return (y * g.astype(jnp.float32)).astype(x.dtype)


def swiglu(x, w_gate, w_up, w_down):
    return (jax.nn.silu(x @ w_gate) * (x @ w_up)) @ w_down


def s5_mixer(u, lam_re, lam_im, log_dt, b_re, b_im, c_re, c_im, d):
    bsz, seq, _ = u.shape
    f32 = jnp.float32
    uf = u.astype(f32).reshape(bsz, seq, S5_G, S5_P)
    lre = lam_re.astype(f32)
    lim = lam_im.astype(f32)
    dt = jnp.exp(log_dt.astype(f32))[:, None]
    mag = jnp.exp(lre * dt)
    ang = lim * dt
    ab_re = mag * jnp.cos(ang)
    ab_im = mag * jnp.sin(ang)
    den = lre * lre + lim * lim
    nr = ab_re - 1.0
    ni = ab_im
    q_re = (nr * lre + ni * lim) / den
    q_im = (ni * lre - nr * lim) / den
    br = b_re.astype(f32)
    bi = b_im.astype(f32)
    bb_re = q_re[..., None] * br - q_im[..., None] * bi
    bb_im = q_re[..., None] * bi + q_im[..., None] * br
    bu_re = jnp.einsum('blgp,gnp->blgn', uf, bb_re)
    bu_im = jnp.einsum('blgp,gnp->blgn', uf, bb_im)
    a_re = jnp.broadcast_to(ab_re, bu_re.shape)
    a_im = jnp.broadcast_to(ab_im, bu_im.shape)

    def combine(e1, e2):
        a1r, a1i, b1r, b1i = e1
        a2r, a2i, b2r, b2i = e2
        return (a1r * a2r - a1i * a2i,
                a1r * a2i + a1i * a2r,
                a2r * b1r - a2i * b1i + b2r,
                a2r * b1i + a2i * b1r + b2i)

    _, _, s_re, s_im = lax.associative_scan(combine, (a_re, a_im, bu_re, bu_im), axis=1)
    y = (jnp.einsum('blgn,gpn->blgp', s_re, c_re.astype(f32))
         - jnp.einsum('blgn,gpn->blgp', s_im, c_im.astype(f32))
         + d.astype(f32) * uf)
    return y.reshape(bsz, seq, S5_G * S5_P).astype(u.dtype)


def causal_depthwise_conv(z, w, b):
    seq = z.shape[1]
    zp = jnp.pad(z, ((0, 0), (CONV_W - 1, 0), (0, 0)))
    out = b
    for k in range(CONV_W):
        out = out + w[k] * zp[:, k:k + seq]
    return out


def hybrid_mixer(x, w_in, lam_re, lam_im, log_dt, b_re, b_im, c_re, c_im, d,
                 w_glu, conv_w, conv_b, g_s5, g_conv, w_out):
    proj = x @ w_in
    u = proj[..., :D_S5]
    gate_b = proj[..., D_S5:D_S5 + D_CONV]
    gate_c = proj[..., D_S5 + D_CONV:D_S5 + 2 * D_CONV]
    h = proj[..., D_S5 + 2 * D_CONV:]
    y = jax.nn.gelu(s5_mixer(u, lam_re, lam_im, log_dt, b_re, b_im, c_re, c_im, d))
    y = y * jax.nn.sigmoid(y @ w_glu)
    y = rms_norm(y, g_s5)
    z = gate_b * causal_depthwise_conv(gate_c * h, conv_w, conv_b)
    z = rms_norm(z, g_conv)
    return jnp.concatenate([y, z], axis=-1) @ w_out


def setup_inputs(seed: int = 0) -> dict:
    key = jax.random.key(seed)
    ks = jax.random.split(key, 32)
    f32 = jnp.float32

    def nrm(k, shape, scale):
        return jax.random.normal(k, shape, f32) * scale

    def gain(k, shape):
        return 1.0 + 0.02 * jax.random.normal(k, shape, f32)

    L = DEPTH
    n_idx = jnp.arange(S5_N, dtype=f32)
    inp = {}
    inp["x"] = jax.random.normal(ks[0], (BATCH, SEQ, D_MODEL), f32)
    inp["ffn1_gate"] = nrm(ks[1], (L, D_MODEL, D_FF), D_MODEL ** -0.5)
    inp["ffn1_up"] = nrm(ks[2], (L, D_MODEL, D_FF), D_MODEL ** -0.5)
    inp["ffn1_down"] = nrm(ks[3], (L, D_FF, D_MODEL), D_FF ** -0.5 * DEEPNORM_BETA)
    inp["ln1_g"] = gain(ks[4], (L, D_MODEL))
    inp["ln1_b"] = nrm(ks[5], (L, D_MODEL), 0.02)
    inp["w_in"] = nrm(ks[6], (L, D_MODEL, D_IN), D_MODEL ** -0.5)
    inp["s5_lam_re"] = -0.5 * jnp.exp(0.05 * jax.random.normal(ks[7], (L, S5_G, S5_N), f32))
    inp["s5_lam_im"] = math.pi * n_idx + 0.01 * jax.random.normal(ks[8], (L, S5_G, S5_N), f32)
    inp["s5_log_dt"] = jax.random.uniform(ks[9], (L, S5_G), f32, math.log(1e-3), math.log(1e-1))
    inp["s5_b_re"] = nrm(ks[10], (L, S5_G, S5_N, S5_P), (2.0 * S5_P) ** -0.5)
    inp["s5_b_im"] = nrm(ks[11], (L, S5_G, S5_N, S5_P), (2.0 * S5_P) ** -0.5)
    inp["s5_c_re"] = nrm(ks[12], (L, S5_G, S5_P, S5_N), (2.0 * S5_N) ** -0.5)
    inp["s5_c_im"] = nrm(ks[13], (L, S5_G, S5_P, S5_N), (2.0 * S5_N) ** -0.5)
    inp["s5_d"] = nrm(ks[14], (L, S5_G, S5_P), 1.0)
    inp["s5_w_glu"] = nrm(ks[15], (L, D_S5, D_S5), D_S5 ** -0.5)
    inp["conv_w"] = nrm(ks[16], (L, CONV_W, D_CONV), CONV_W ** -0.5)
    inp["conv_b"] = nrm(ks[17], (L, D_CONV), 0.02)
    inp["g_s5"] = gain(ks[18], (L, D_S5))
    inp["g_conv"] = gain(ks[19], (L, D_CONV))
    inp["w_out"] = nrm(ks[20], (L, D_MIX, D_MODEL), D_MIX ** -0.5 * DEEPNORM_BETA)
    inp["ln2_g"] = gain(ks[21], (L, D_MODEL))
    inp["ln2_b"] = nrm(ks[22], (L, D_MODEL), 0.02)
    inp["ffn2_gate"] = nrm(ks[23], (L, D_MODEL, D_FF), D_MODEL ** -0.5)
    inp["ffn2_up"] = nrm(ks[24], (L, D_MODEL, D_FF), D_MODEL ** -0.5)
    inp["ffn2_down"] = nrm(ks[25], (L, D_FF, D_MODEL), D_FF ** -0.5 * DEEPNORM_BETA)
    inp["ln3_g"] = gain(ks[26], (L, D_MODEL))
    inp["ln3_b"] = nrm(ks[27], (L, D_MODEL), 0.02)
    return inp


def reference(x, ffn1_gate, ffn1_up, ffn1_down, ln1_g, ln1_b, w_in, s5_lam_re, s5_lam_im,
              s5_log_dt, s5_b_re, s5_b_im, s5_c_re, s5_c_im, s5_d, s5_w_glu, conv_w, conv_b,
              g_s5, g_conv, w_out, ln2_g, ln2_b, ffn2_gate, ffn2_up, ffn2_down, ln3_g, ln3_b):
    for l in range(DEPTH):
        x = layer_norm(DEEPNORM_ALPHA * x + 0.5 * swiglu(x, ffn1_gate[l], ffn1_up[l], ffn1_down[l]),
                       ln1_g[l], ln1_b[l])
        m = hybrid_mixer(x, w_in[l], s5_lam_re[l], s5_lam_im[l], s5_log_dt[l], s5_b_re[l],
                         s5_b_im[l], s5_c_re[l], s5_c_im[l], s5_d[l], s5_w_glu[l], conv_w[l],
                         conv_b[l], g_s5[l], g_conv[l], w_out[l])
        x = layer_norm(DEEPNORM_ALPHA * x + m, ln2_g[l], ln2_b[l])
        x = layer_norm(DEEPNORM_ALPHA * x + 0.5 * swiglu(x, ffn2_gate[l], ffn2_up[l], ffn2_down[l]),
                       ln3_g[l], ln3_b[l])
    return x
```

```python
import math
from contextlib import ExitStack
import numpy as np
import concourse.bass as bass
import concourse.mybir as mybir
from concourse.bass_utils import run_bass_kernel_spmd

F32 = mybir.dt.float32
BF16 = mybir.dt.bfloat16
AF = mybir.ActivationFunctionType
ALU = mybir.AluOpType

D = 2048
KC = D // 128
TB = 512
DIN = 4096
ALPHA = (2.0 * 4) ** 0.25
LN_EPS = 1e-5
RMS_EPS = 1e-6
NCORES = 8
PAIR_GROUPS = [[0, 1], [2, 3], [4, 5], [6, 7]]


class Tok:
    __slots__ = ("sem", "val", "eng")

    def __init__(self, sem, val, eng):
        self.sem, self.val, self.eng = sem, val, eng


class Sched:
    def __init__(self, nc, es):
        self.nc = nc
        self.e = {}
        for name, obj in (("pe", nc.tensor), ("act", nc.scalar), ("dve", nc.vector),
                          ("pool", nc.gpsimd), ("sp", nc.sync)):
            sem = es.enter_context(nc.semaphore("s_" + name))
            self.e[name] = dict(obj=obj, sem=sem, cnt=0, waited={})
        self.nds = 6
        self.dsem = {q: [es.enter_context(nc.semaphore("d_%s%d" % (q, i))) for i in range(self.nds)]
                     for q in ("sp", "pool")}
        self.dcnt = {q: [0] * self.nds for q in ("sp", "pool")}
        self.drr = {"sp": 0, "pool": 0}
        self.lastw = {}
        self.readers = {}
        self.ccsem = es.enter_context(nc.semaphore("s_cc"))

    def _wait(self, en, tok):
        if tok is None:
            return
        if getattr(self, "skip_eng", None) is not None and tok.eng == self.skip_eng:
            return
        e = self.e[en]
        k = id(tok.sem)
        if e["waited"].get(k, 0) >= tok.val:
            return
        e["obj"].wait_ge(tok.sem, tok.val)
        e["waited"][k] = tok.val

    def _deps(self, en, reads, writes):
        for k in reads:
            self._wait(en, self.lastw.get(k))
        for k in writes:
            self._wait(en, self.lastw.get(k))
            for t in self.readers.get(k, {}).values():
                self._wait(en, t)

    def _record(self, tok, reads, writes, rk):
        for k in reads:
            self.readers.setdefault(k, {})[rk] = tok
        for k in writes:
            self.lastw[k] = tok
            self.readers[k] = {}

    def op(self, en, fn, reads=(), writes=(), mark=True, nosame=False):
        self.skip_eng = en if nosame else None
        self._deps(en, reads, writes)
        self.skip_eng = None
        ins = fn(self.e[en]["obj"])
        if mark:
            e = self.e[en]
            e["cnt"] += 1
            ins.then_inc(e["sem"], 1)
            tok = Tok(e["sem"], e["cnt"], en)
            self._record(tok, reads, writes, en)
        return ins

    def dma(self, q, out, in_, reads=(), writes=()):
        self._deps(q, reads, writes)
        j = self.drr[q]
        self.drr[q] = (j + 1) % self.nds
        sem = self.dsem[q][j]
        if self.dcnt[q][j] > 0:
            self._wait(q, Tok(sem, self.dcnt[q][j], q))
        self.e[q]["obj"].dma_start(out=out, in_=in_).then_inc(sem, 16)
        self.dcnt[q][j] += 16
        tok = Tok(sem, self.dcnt[q][j], q)
        self._record(tok, reads, writes, "%s%d" % (q, j))
        return tok

    def cc(self, fn, reads=(), writes=()):
        self._deps("pool", reads, writes)
        self.ccn = getattr(self, "ccn", 0) + 1
        fn(self.e["pool"]["obj"]).then_inc(self.ccsem)
        tok = Tok(self.ccsem, self.ccn, "pool")
        self._record(tok, reads, writes, "cc")
        return tok

    def barrier(self):
        toks = []
        for en, e in self.e.items():
            if e["cnt"] > 0:
                toks.append(Tok(e["sem"], e["cnt"], en))
        for q in ("sp", "pool"):
            for j in range(self.nds):
                if self.dcnt[q][j] > 0:
                    toks.append(Tok(self.dsem[q][j], self.dcnt[q][j], q))
        if getattr(self, "ccn", 0) > 0:
            toks.append(Tok(self.ccsem, self.ccn, "pool"))
        for en in self.e:
            for t in toks:
                self._wait(en, t)
        self.lastw = {}
        self.readers = {}


def build_nc(L=4, T=1024, FF=5632, stop_after=None, debug=False, groups=None):
    nc = bass.Bass("TRN2", target_bir_lowering=False)
    groups = groups or PAIR_GROUPS
    NFT = FF // 128
    NTB = T // TB
    TS = 2 * T
    NSB = TS // TB
    NLEV = int(round(math.log2(TS)))
    NP = 16

    def din(name, shape):
        return nc.dram_tensor(name, list(shape), F32, kind="ExternalInput").ap()

    x_in = din("x", (T, D))
    w_f1g = din("ffn1_gate", (L, D, FF)); w_f1u = din("ffn1_up", (L, D, FF)); w_f1d = din("ffn1_down", (L, FF, D))
    w_f2g = din("ffn2_gate", (L, D, FF)); w_f2u = din("ffn2_up", (L, D, FF)); w_f2d = din("ffn2_down", (L, FF, D))
    w_in_u = din("w_in_u", (L, D, 1024))
    flag_in = din("flag", (128, 2))
    dloc_in = din("dloc", (128, L, 4))
    w_in = din("w_in", (L, D, DIN)); w_glu = din("s5_w_glu", (L, 1024, 1024)); w_out = din("w_out", (L, D, D))
    lnp = din("lnp", (L, 6, D))
    s5s = din("s5s", (128, L, 3, 16))
    s5b = din("s5b", (L, 2, 128, 16, 128))
    s5c = din("s5c", (L, 2, 128, 16, 128))
    chp = din("chp", (128, L, 7, 8))
    ident_in = din("ident", (128, 128))
    y_out = nc.dram_tensor("y", [T, D], F32, kind="ExternalOutput").ap()
    xr = nc.dram_tensor("xr", [T, D], F32).ap()
    if debug:
        dbg_u = nc.dram_tensor("dbg_u", [128, 4, TS], BF16, kind="ExternalOutput").ap()
        dbg_ys = nc.dram_tensor("dbg_ys", [128, 4, TS], BF16, kind="ExternalOutput").ap()
        dbg_yg = nc.dram_tensor("dbg_yg", [128, 8, T], BF16, kind="ExternalOutput").ap()
        dbg_y3 = nc.dram_tensor("dbg_y3", [128, 16, TB], BF16, kind="ExternalOutput").ap()
        dbg_rst = nc.dram_tensor("dbg_rst", [128, 2, 4], F32, kind="ExternalOutput").ap()
        dbg_halo = nc.dram_tensor("dbg_halo", [128, 8, 2], F32, kind="ExternalOutput").ap()
    cu_in = nc.dram_tensor("cu_in", [1024, T], BF16)
    cu_out = nc.dram_tensor("cu_out", [2048, T], BF16)
    cy_in = nc.dram_tensor("cy_in", [512, TS], BF16)
    cy_out = nc.dram_tensor("cy_out", [1024, TS], BF16)
    ch_in = nc.dram_tensor("ch_in", [128, 16], BF16)
    ch_out = nc.dram_tensor("ch_out", [256, 16], BF16)

    es = ExitStack()
    with es:
        S = Sched(nc, es)

        uniq = [0]

        def sb(name, shape, dt, stack=es):
            uniq[0] += 1
            return stack.enter_context(nc.sbuf_tensor("%s_%d" % (name, uniq[0]), list(shape), dt))

        def ps(name, shape, dt=F32, stack=es):
            return stack.enter_context(nc.psum_tensor(name, list(shape), dt))

        block = es.enter_context(nc.Block())

        @block.gpsimd
        def _(_g):
            xT = sb("xT", (128, KC, T), BF16)
            identb = sb("identb", (128, 128), BF16)
            identf = sb("identf", (128, 128), F32)
            onesb = sb("onesb", (128, 1), BF16)
            chp_sb = sb("chp_sb", (128, L, 7, 8), F32)
            s5s_sb = sb("s5s_sb", (128, L, 3, 16), F32)
            flag_sb = sb("flag_sb", (128, 2), F32)
            dloc_sb = sb("dloc_sb", (128, L, 4), F32)
            gb_sb = sb("gb_sb", (128, 2, D), F32)
            lnst = sb("lnst", (128, 4, 6), F32)
            lnmv = sb("lnmv", (128, 4), F32)
            pst = ps("pst", (128, 1024), BF16)
            psA = [ps("psA%d" % i, (128, 512)) for i in range(4)]
            psB = [ps("psB%d" % i, (128, 512)) for i in range(2)]
            psS = ps("psS", (128, 512))

            S.dma("sp", identf[:], ident_in, writes=["identf"])
            S.dma("pool", identb[:], ident_in, writes=["identb"])
            S.dma("sp", chp_sb[:], chp, writes=["chp"])
            S.dma("sp", s5s_sb[:], s5s, writes=["s5s"])
            S.dma("sp", flag_sb[:], flag_in, writes=["flag"])
            S.dma("sp", dloc_sb[:], dloc_in, writes=["dloc"])
            S.op("dve", lambda v: v.memset(onesb[:], 1.0), writes=["onesb"])

            def load_gb(l, which):
                for i in range(2):
                    S.dma("sp", gb_sb[:, i, :], lnp[l, 2 * which + i, :].partition_broadcast(128),
                          writes=["gb"])

            def to_xT(src, srck, tt, xb, xbk):
                S.op("act", lambda a: a.activation(out=xb, in_=src, func=AF.Copy),
                     reads=[srck], writes=xbk)
                for half in range(2):
                    for j in range(8):
                        dc = half * 8 + j
                        S.op("pe", lambda p, dc=dc, j=j: p.transpose(pst[:, j * 128:(j + 1) * 128],
                                                                   xb[:, dc * 128:(dc + 1) * 128], identb[:]),
                             reads=xbk + ["identb"], writes=["pst"], mark=(j == 7))
                    S.op("act", lambda a, half=half: a.activation(
                        out=xT[:, half * 8:(half + 1) * 8, tt * 128:(tt + 1) * 128],
                        in_=pst[:].rearrange("p (c t) -> p c t", c=8), func=AF.Copy),
                        reads=["pst"], writes=["xT%d" % (tt // 4)])

            def layer_norm_tile(v, vk, tt, eps, dst, xb, xbk):
                for c in range(4):
                    S.op("dve", lambda e, c=c: e.bn_stats(lnst[:, c, :], v[:, c * 512:(c + 1) * 512]),
                         reads=[vk], writes=["lnst"], mark=(c == 3))
                S.op("dve", lambda e: e.bn_aggr(lnmv[:, 0:2], lnst[:].rearrange("p a b -> p (a b)")),
                     reads=["lnst"], writes=["lnmv"])
                S.op("dve", lambda e: e.tensor_scalar(lnmv[:, 2:3], lnmv[:, 1:2], float(eps), None, ALU.add),
                     reads=["lnmv"], writes=["lnmv"])
                S.op("act", lambda a: a.activation(out=lnmv[:, 2:3], in_=lnmv[:, 2:3], func=AF.Sqrt),
                     reads=["lnmv"], writes=["lnmv"])
                S.op("dve", lambda e: e.reciprocal(lnmv[:, 2:3], lnmv[:, 2:3]),
                     reads=["lnmv"], writes=["lnmv"])
                S.op("dve", lambda e: e.tensor_scalar(v, v, lnmv[:, 0:1], lnmv[:, 2:3],
                                                      ALU.subtract, ALU.mult),
                     reads=[vk, "lnmv"], writes=[vk])
                S.op("dve", lambda e: e.tensor_tensor(v, v, gb_sb[:, 0, :], ALU.mult),
                     reads=[vk, "gb"], writes=[vk])
                S.op("dve", lambda e: e.tensor_tensor(v, v, gb_sb[:, 1, :], ALU.add),
                     reads=[vk, "gb"], writes=[vk])
                S.dma("sp", dst[tt * 128:(tt + 1) * 128, :], v, reads=[vk], writes=["xr%d" % tt])
                to_xT(v, vk, tt, xb, xbk)

            with ExitStack() as st:
                x0 = [sb("x0_%d" % i, (128, D), F32, st) for i in range(2)]
                xb0 = sb("xb0", (128, D), BF16, st)
                for tt in range(T // 128):
                    b = tt % 2
                    S.dma("sp", x0[b][:], x_in[tt * 128:(tt + 1) * 128, :], writes=["x0_%d" % b])
                    to_xT(x0[b][:], "x0_%d" % b, tt, xb0[:], ["xb"])
            S.barrier()

            def ffn_stage(l, wg, wu, wd, lnidx, src, dst):
                load_gb(l, lnidx)
                c1 = 0.5 / ALPHA
                eps = LN_EPS / (ALPHA * ALPHA)
                with ExitStack() as st:
                    hT = sb("hT", (128, NFT, TB), BF16, st)
                    R = sb("R", (128, 4, 2 * KC * 256), BF16, st)
                    wgu = [R[:, i, :].rearrange("p (w k c) -> p w k c", w=2, k=KC) for i in range(4)]
                    wdb = [R[:, 2 * i:2 * i + 2, :].rearrange("p a b -> p (a b)")[:, 0:NFT * 256]
                           .rearrange("p (f c) -> p f c", c=256) for i in range(2)]
                    wdk = [["R0", "R1"], ["R2", "R3"]]
                    xbf = sb("xbf", (128, D), BF16, st)
                    xres = sb("xres", (128, 4, D), F32, st)
                    sgx = sb("sgx", (128, 2 * TB), F32, st)
                    sg = [sgx[:, 0:TB], sgx[:, TB:2 * TB]]
                    wgv = wg[l].rearrange("(kc p) f -> p kc f", p=128)
                    wuv = wu[l].rearrange("(kc p) f -> p kc f", p=128)
                    wdv = wd[l].rearrange("(fc p) d -> p fc d", p=128)
                    def loadx(tb):
                        for tt in range(4):
                            g = tb * 4 + tt
                            S.dma("sp", xres[:, tt, :], src[g * 128:(g + 1) * 128, :],
                                  reads=["xr%d" % g], writes=["xres%d" % tt])

                    def up(tb):
                        tsl = slice(tb * TB, (tb + 1) * TB)
                        for ft in range(NFT):
                            wb = (ft // 2) % 4
                            sub = ft % 2
                            b = ft % 2
                            if sub == 0:
                                c0 = (ft // 2) * 256
                                S.dma("pool", wgu[wb][:, 0], wgv[:, :, c0:c0 + 256], writes=["R%d" % wb])
                                S.dma("pool", wgu[wb][:, 1], wuv[:, :, c0:c0 + 256], writes=["R%d" % wb])
                            for which in range(2):
                                pt = psA[2 * b + which]
                                for kc in range(KC):
                                    S.op("pe", lambda p, pt=pt, which=which, kc=kc, wb=wb, sub=sub: p.matmul(
                                        pt[:], wgu[wb][:, which, kc, sub * 128:(sub + 1) * 128], xT[:, kc, tsl],
                                        start=(kc == 0), stop=(kc == KC - 1)),
                                        reads=["R%d" % wb, "xT%d" % tb], writes=["psA%d" % (2 * b + which)],
                                        mark=(kc == KC - 1))
                            S.op("act", lambda a, b=b: a.activation(out=sg[b][:], in_=psA[2 * b][:], func=AF.Silu),
                                 reads=["psA%d" % (2 * b)], writes=["sg%d" % b])
                            S.op("dve", lambda v, b=b, ft=ft: v.tensor_tensor(hT[:, ft, :], sg[b][:],
                                                                              psA[2 * b + 1][:], ALU.mult),
                                 reads=["sg%d" % b, "psA%d" % (2 * b + 1)], writes=["hT"])
                    def down(tb):
                        for dt in range(D // 128):
                            b = dt % 2
                            wb = (dt // 2) % 2
                            sub = dt % 2
                            if sub == 0:
                                c0 = (dt // 2) * 256
                                S.dma("pool", wdb[wb], wdv[:, :, c0:c0 + 256], writes=wdk[wb])
                            for fc in range(NFT):
                                S.op("pe", lambda p, fc=fc, b=b, wb=wb, sub=sub: p.matmul(
                                    psB[b][:], wdb[wb][:, fc, sub * 128:(sub + 1) * 128], hT[:, fc, :],
                                    start=(fc == 0), stop=(fc == NFT - 1)),
                                    reads=["hT"] + wdk[wb], writes=["psB%d" % b], mark=(fc == NFT - 1))
                            S.op("act", lambda a, b=b: a.activation(out=sg[b][:], in_=psB[b][:], func=AF.Copy, scale=float(c1)),
                                 reads=["psB%d" % b], writes=["sg%d" % b])
                            if dt > 0:
                                back(dt - 1)
                        back(D // 128 - 1)

                    def back(dt):
                        b = dt % 2
                        for tt in range(4):
                            S.op("pe", lambda p, tt=tt, b=b: p.transpose(
                                psA[b][:, tt * 128:(tt + 1) * 128], sg[b][:, tt * 128:(tt + 1) * 128], identf[:]),
                                reads=["sg%d" % b, "identf"], writes=["psA%d" % b], mark=(tt == 3))
                        S.op("dve", lambda v, dt=dt, b=b: v.tensor_tensor(
                            xres[:, :, dt * 128:(dt + 1) * 128], xres[:, :, dt * 128:(dt + 1) * 128],
                            psA[b][:].rearrange("p (t d) -> p t d", t=4), ALU.add),
                            reads=["psA%d" % b] + ["xres%d" % t_ for t_ in range(4)],
                            writes=["xres%d" % t_ for t_ in range(4)])
                    def lnorm(tb):
                        for tt in range(4):
                            layer_norm_tile(xres[:, tt, :], "xres%d" % tt, tb * 4 + tt, eps, dst, xbf[:], ["xbf"])

                    for tb in range(NTB):
                        up(tb)
                        if tb > 0:
                            lnorm(tb - 1)
                        loadx(tb)
                        down(tb)
                    lnorm(NTB - 1)
                S.barrier()

            def mixer_stage(l):
                load_gb(l, 1)
                w_in_v = w_in[l].rearrange("(kc p) f -> p kc f", p=128)
                w_in_uv = w_in_u[l].rearrange("(kc p) f -> p kc f", p=128)
                cp = lambda which, j: chp_sb[:, l, which, j:j + 1]
                fl = flag_sb[:, 0:1]
                nfl = flag_sb[:, 1:2]
                with ExitStack() as st:
                    chh = sb("chh", (128, 8, 2 + T), BF16, st)
                    gbs = sb("gbs", (128, 8, T), BF16, st)
                    halo = sb("halo", (128, 8, 2), BF16, st)
                    with ExitStack() as stU:
                        u = sb("u", (128, 4, TS), BF16, stU)
                        coef = sb("coef", (128, NP, NLEV, 3), F32, stU)
                        prm = sb("prm", (128, 12, NP), F32, stU)
                        lhsB = sb("lhsB", (128, NP, 2, 128), BF16, stU)
                        lhsC = sb("lhsC", (128, NP, 2, 128), BF16, stU)
                        with ExitStack() as st2:
                            wt = [sb("wt%d" % i, (128, KC, 128), BF16, st2) for i in range(2)]
                            ust = sb("ust", (128, 8, T), BF16, st2)
                            ua = sb("ua", (128, 4, T), BF16, st2)
                            ub = sb("ub", (128, 4, T), BF16, st2)
                            for j in range(8):
                                b = j % 2
                                S.dma("pool", wt[b][:], w_in_uv[:, :, j * 128:(j + 1) * 128], writes=["wt%d" % b])
                                for tb in range(NTB):
                                    pt = psA[tb]
                                    for kc in range(KC):
                                        S.op("pe", lambda p, pt=pt, kc=kc, b=b, tb=tb: p.matmul(
                                            pt[:], wt[b][:, kc, :], xT[:, kc, tb * TB:(tb + 1) * TB],
                                            start=(kc == 0), stop=(kc == KC - 1)),
                                            reads=["wt%d" % b, "xT%d" % tb], writes=["psA%d" % tb], mark=(kc == KC - 1))
                                    S.op("act", lambda a, pt=pt, j=j, tb=tb: a.activation(
                                        out=ust[:, j, tb * TB:(tb + 1) * TB], in_=pt[:], func=AF.Copy),
                                        reads=["psA%d" % tb], writes=["ust"])
                            S.dma("sp", cu_in.ap().rearrange("(k p) t -> p k t", p=128), ust[:],
                                  reads=["ust"], writes=["cu_in"])
                            S.cc(lambda g: g.collective_compute("AllGather", ALU.bypass, replica_groups=groups,
                                                                ins=[cu_in.ap().opt()], outs=[cu_out.ap().opt()]),
                                 reads=["cu_in"], writes=["cu_out"])
                            lre = s5s_sb[:, l, 0, :]; lim = s5s_sb[:, l, 1, :]; ldt = s5s_sb[:, l, 2, :]
                            P = lambda i: prm[:, i, :]
                            V = lambda fn, r=("prm", "s5s"), w=("prm",): S.op("dve", fn, reads=list(r), writes=list(w))
                            S.op("act", lambda a: a.activation(out=P(0), in_=ldt, func=AF.Exp),
                                 reads=["s5s", "prm"], writes=["prm"])
                            V(lambda v: v.tensor_tensor(P(1), lre, P(0), ALU.mult))
                            V(lambda v: v.tensor_tensor(P(2), lim, P(0), ALU.mult))
                            S.op("act", lambda a: a.activation(out=P(3), in_=P(1), func=AF.Exp),
                                 reads=["prm"], writes=["prm"])
                            V(lambda v: v.tensor_scalar(P(5), P(2), 1.0 / 32.0, 0.5 * math.pi, ALU.mult, ALU.add))
                            S.op("act", lambda a: a.activation(out=P(4), in_=P(2), func=AF.Sin, scale=1.0 / 32.0),
                                 reads=["prm"], writes=["prm"])
                            S.op("act", lambda a: a.activation(out=P(5), in_=P(5), func=AF.Sin),
                                 reads=["prm"], writes=["prm"])

                            def csq(cr, ci):
                                V(lambda v: v.tensor_tensor(P(7), cr, cr, ALU.mult), r=("prm", "coef"), w=("prm", "coef"))
                                V(lambda v: v.tensor_tensor(P(8), ci, ci, ALU.mult), r=("prm", "coef"), w=("prm", "coef"))
                                V(lambda v: v.tensor_tensor(P(7), P(7), P(8), ALU.subtract), r=("prm", "coef"), w=("prm", "coef"))
                                V(lambda v: v.tensor_tensor(P(8), cr, ci, ALU.mult), r=("prm", "coef"), w=("prm", "coef"))
                                return P(7), P(8)
                            for _i in range(5):
                                re_, imh = csq(P(5), P(4))
                                V(lambda v: v.tensor_copy(out=P(5), in_=re_))
                                V(lambda v: v.tensor_scalar(P(4), imh, 2.0, None, ALU.mult))
                            V(lambda v: v.tensor_tensor(coef[:, :, 0, 0], P(3), P(5), ALU.mult), w=("prm", "coef"))
                            V(lambda v: v.tensor_tensor(coef[:, :, 0, 1], P(3), P(4), ALU.mult), w=("prm", "coef"))
                            for k in range(1, NLEV):
                                re_, imh = csq(coef[:, :, k - 1, 0], coef[:, :, k - 1, 1])
                                V(lambda v, k=k: v.tensor_copy(out=coef[:, :, k, 0], in_=re_), r=("prm", "coef"), w=("prm", "coef"))
                                V(lambda v, k=k: v.tensor_scalar(coef[:, :, k, 1], imh, 2.0, None, ALU.mult),
                                  r=("prm", "coef"), w=("prm", "coef"))
                            for k in range(NLEV):
                                V(lambda v, k=k: v.tensor_scalar(coef[:, :, k, 2], coef[:, :, k, 1], -1.0, None, ALU.mult),
                                  r=("prm", "coef"), w=("prm", "coef"))
                            V(lambda v: v.tensor_scalar(P(3), coef[:, :, 0, 0], -1.0, None, ALU.add), r=("prm", "coef"))
                            V(lambda v: v.tensor_tensor(P(6), lre, lre, ALU.mult))
                            V(lambda v: v.tensor_tensor(P(7), lim, lim, ALU.mult))
                            V(lambda v: v.tensor_tensor(P(6), P(6), P(7), ALU.add))
                            V(lambda v: v.reciprocal(P(6), P(6)))
                            V(lambda v: v.tensor_tensor(P(7), P(3), lre, ALU.mult))
                            V(lambda v: v.tensor_tensor(P(8), coef[:, :, 0, 1], lim, ALU.mult), r=("prm", "coef", "s5s"))
                            V(lambda v: v.tensor_tensor(P(7), P(7), P(8), ALU.add))
                            V(lambda v: v.tensor_tensor(P(9), P(7), P(6), ALU.mult))
                            V(lambda v: v.tensor_tensor(P(7), coef[:, :, 0, 1], lre, ALU.mult), r=("prm", "coef", "s5s"))
                            V(lambda v: v.tensor_tensor(P(8), P(3), lim, ALU.mult))
                            V(lambda v: v.tensor_tensor(P(7), P(7), P(8), ALU.subtract))
                            V(lambda v: v.tensor_tensor(P(10), P(7), P(6), ALU.mult))
                            with ExitStack() as st3:
                                bre = sb("bre", (128, 8, 128), F32, st3)
                                bim = sb("bim", (128, 8, 128), F32, st3)
                                t1 = sb("t1", (128, 8, 128), F32, st3)
                                t2 = sb("t2", (128, 8, 128), F32, st3)
                                for c8 in range(NP // 8):
                                    psl = slice(c8 * 8, (c8 + 1) * 8)
                                    S.dma("sp", bre[:], s5b[l, 0, :, psl, :], writes=["bre"])
                                    S.dma("sp", bim[:], s5b[l, 1, :, psl, :], writes=["bim"])
                                    qre = prm[:, 9, psl].unsqueeze(2).to_broadcast([128, 8, 128])
                                    qim = prm[:, 10, psl].unsqueeze(2).to_broadcast([128, 8, 128])
                                    W = lambda fn, r, w: S.op("dve", fn, reads=list(r) + ["prm"], writes=list(w))
                                    W(lambda v: v.tensor_tensor(t1[:], bre[:], qre, ALU.mult), ["bre"], ["t1"])
                                    W(lambda v: v.tensor_tensor(t2[:], bim[:], qim, ALU.mult), ["bim"], ["t2"])
                                    W(lambda v: v.tensor_tensor(t1[:], t1[:], t2[:], ALU.subtract), ["t1", "t2"], ["t1"])
                                    W(lambda v: v.tensor_tensor(t2[:], bim[:], qre, ALU.mult), ["bim", "t1"], ["t2"])
                                    W(lambda v: v.tensor_tensor(bim[:], bre[:], qim, ALU.mult), ["bre"], ["bim"])
                                    W(lambda v: v.tensor_tensor(t2[:], t2[:], bim[:], ALU.add), ["t2", "bim"], ["t2"])
                                    for pl, src_ in ((0, t1), (1, t2)):
                                        for g4 in range(2):
                                            for i in range(4):
                                                pp = g4 * 4 + i
                                                S.op("pe", lambda p, src_=src_, pp=pp, i=i: p.transpose(
                                                    psS[:, i * 128:(i + 1) * 128], src_[:, pp, :], identf[:]),
                                                    reads=["t1", "t2", "identf"], writes=["psS"], mark=(i == 3))
                                            S.op("act", lambda a, pl=pl, g4=g4, c8=c8: a.activation(
                                                out=lhsB[:, c8 * 8 + g4 * 4:c8 * 8 + g4 * 4 + 4, pl, :],
                                                in_=psS[:].rearrange("p (a b) -> p a b", a=4), func=AF.Copy),
                                                reads=["psS"], writes=["lhsB"])
                                    S.dma("sp", bre[:], s5c[l, 0, :, psl, :], reads=[], writes=["bre"])
                                    S.dma("sp", bim[:], s5c[l, 1, :, psl, :], reads=[], writes=["bim"])
                                    S.op("act", lambda a, psl=psl: a.activation(out=lhsC[:, psl, 0, :], in_=bre[:], func=AF.Copy),
                                         reads=["bre"], writes=["lhsC"])
                                    S.op("act", lambda a, psl=psl: a.activation(out=lhsC[:, psl, 1, :], in_=bim[:], func=AF.Copy,
                                                                              scale=-1.0),
                                         reads=["bim"], writes=["lhsC"])
                            cuv = cu_out.ap().rearrange("(r k p) t -> r p k t", r=2, p=128)
                            for r in range(2):
                                S.dma("sp", ua[:], cuv[r, :, 0:4, :], reads=["cu_out"], writes=["ua"])
                                S.dma("sp", ub[:], cuv[r, :, 4:8, :], reads=["cu_out"], writes=["ub"])
                                fa, fb = (nfl, fl) if r == 0 else (fl, nfl)
                                S.op("dve", lambda v, fa=fa: v.tensor_scalar(ua[:], ua[:], fa, None, ALU.mult),
                                     reads=["ua", "flag"], writes=["ua"])
                                S.op("dve", lambda v, fb=fb, r=r: v.scalar_tensor_tensor(
                                    u[:, :, r * T:(r + 1) * T], ub[:], fb, ua[:], ALU.mult, ALU.add),
                                    reads=["ua", "ub", "flag"], writes=["u%d" % k_ for k_ in range(4)])
                        S.barrier()
                        with ExitStack() as st2:
                            S.barrier()
                            Zr2 = [sb("Zr%d" % i, (128, TS), F32, st2) for i in range(2)]
                            Zi2 = [sb("Zi%d" % i, (128, TS), F32, st2) for i in range(2)]
                            sbf2 = [sb("sbf%d" % i, (128, 2, TS), BF16, st2) for i in range(2)]
                            yv = sb("yv", (128, TB), F32, st2)
                            y2 = sb("y2", (128, TB), F32, st2)
                            wtc = [sb("wtc%d" % i, (128, 3, KC, 128), BF16, st2) for i in range(2)]
                            gcs = sb("gcs", (128, TB), F32, st2)

                            def emit_bu(pr):
                                k8 = pr // 4
                                zb = pr % 2
                                for pl, Z in ((0, Zr2[zb]), (1, Zi2[zb])):
                                    for tb in range(NSB):
                                        S.op("pe", lambda p, pl=pl, tb=tb: p.matmul(
                                            psB[0][:], lhsB[:, pr, pl, :], u[:, k8, tb * TB:(tb + 1) * TB],
                                            start=True, stop=True),
                                            reads=["lhsB", "u%d" % k8], writes=["psB0"])
                                        S.op("act", lambda a, Z=Z, tb=tb: a.activation(
                                            out=Z[:, tb * TB:(tb + 1) * TB], in_=psB[0][:], func=AF.Copy),
                                            reads=["psB0"], writes=["Z%d" % zb])

                            def conv_proj(ci):
                                tb, j = ci // 8, ci % 8
                                b = ci % 2
                                tsl = slice(tb * TB, (tb + 1) * TB)
                                for gi in range(3):
                                    c0 = 1024 * (gi + 1) + j * 128
                                    S.dma("pool", wtc[b][:, gi], w_in_v[:, :, c0:c0 + 128], writes=["wtc%d" % b])

                                def proj(gi, pt, ptk):
                                    for kc in range(KC):
                                        S.op("pe", lambda p, kc=kc: p.matmul(
                                            pt[:], wtc[b][:, gi, kc, :], xT[:, kc, tsl],
                                            start=(kc == 0), stop=(kc == KC - 1)),
                                            reads=["wtc%d" % b, "xT%d" % tb], writes=[ptk], mark=(kc == KC - 1))
                                proj(1, psB[1], "psB1")
                                S.op("act", lambda a: a.activation(out=gcs[:], in_=psB[1][:], func=AF.Copy),
                                     reads=["psB1"], writes=["gcs"])
                                proj(2, psS, "psS")
                                proj(0, psB[1], "psB1")
                                S.op("act", lambda a: a.activation(out=gbs[:, j, tsl], in_=psB[1][:], func=AF.Copy),
                                     reads=["psB1"], writes=["gbs"])

                            def conv_mul(ci):
                                tb, j = ci // 8, ci % 8
                                S.op("dve", lambda v: v.tensor_tensor(chh[:, j, 2 + tb * TB:2 + (tb + 1) * TB], gcs[:], psS[:], ALU.mult),
                                     reads=["gcs", "psS"], writes=["chh"])

                            def scan_pair(pr):
                                k8, r = pr // 4, pr % 4
                                zb = pr % 2
                                Zr, Zi, sbf = Zr2[zb], Zi2[zb], sbf2[zb]
                                zk = "Z%d" % zb
                                sk = "sbf%d" % zb

                                def upd(k, dsl, ssl, last=False):
                                    ar = coef[:, pr, k, 0:1]; ai = coef[:, pr, k, 1:2]; nai = coef[:, pr, k, 2:3]
                                    ops = ((Zr, Zr, ar), (Zi, Zi, ar), (Zr, Zi, nai), (Zi, Zr, ai))
                                    for n_, (dz, sz, cc) in enumerate(ops):
                                        S.op("dve", lambda v, dz=dz, sz=sz, cc=cc: v.scalar_tensor_tensor(
                                            dz[:, dsl], sz[:, ssl], cc, dz[:, dsl], ALU.mult, ALU.add),
                                            reads=[zk, "coef"], writes=[zk], nosame=True,
                                            mark=(last and n_ == 3))
                                for k in range(NLEV):
                                    st_ = 2 ** (k + 1)
                                    upd(k, slice(st_ - 1, TS, st_), slice(2 ** k - 1, TS, st_))
                                for k in range(NLEV - 2, -1, -1):
                                    st_ = 2 ** (k + 1)
                                    upd(k, slice(st_ + 2 ** k - 1, TS, st_), slice(st_ - 1, TS - 2 ** k, st_), last=(k == 0))
                                S.op("act", lambda a: a.activation(out=sbf[:, 0, :], in_=Zr[:], func=AF.Copy),
                                     reads=[zk], writes=[sk])
                                S.op("act", lambda a: a.activation(out=sbf[:, 1, :], in_=Zi[:], func=AF.Copy),
                                     reads=[zk], writes=[sk])
                                for tb in range(NSB):
                                    for pl in range(2):
                                        S.op("pe", lambda p, tb=tb, pl=pl: p.matmul(
                                            psA[tb][:], lhsC[:, pr, pl, :], sbf[:, pl, tb * TB:(tb + 1) * TB],
                                            start=(r == 0 and pl == 0), stop=(r == 3 and pl == 1)),
                                            reads=["lhsC", sk], writes=["psA%d" % tb], mark=True)

                            def epilogue(k8):
                                for tb in range(NSB):
                                    tsl = slice(tb * TB, (tb + 1) * TB)
                                    S.op("dve", lambda v, tb=tb, tsl=tsl: v.scalar_tensor_tensor(
                                        yv[:], u[:, k8, tsl], dloc_sb[:, l, k8:k8 + 1], psA[tb][:], ALU.mult, ALU.add),
                                        reads=["u%d" % k8, "psA%d" % tb, "dloc"], writes=["yv"])
                                    S.op("dve", lambda v: v.tensor_tensor(y2[:], yv[:], yv[:], ALU.mult),
                                         reads=["yv"], writes=["y2"])
                                    S.op("dve", lambda v: v.tensor_scalar(y2[:], y2[:], 0.044715, 1.0, ALU.mult, ALU.add),
                                         reads=["y2"], writes=["y2"])
                                    S.op("dve", lambda v: v.tensor_tensor(y2[:], y2[:], yv[:], ALU.mult),
                                         reads=["y2", "yv"], writes=["y2"])
                                    S.op("act", lambda a: a.activation(out=y2[:], in_=y2[:], func=AF.Sigmoid,
                                                                       scale=2.0 * math.sqrt(2.0 / math.pi)),
                                         reads=["y2"], writes=["y2"])
                                    S.op("dve", lambda v, tsl=tsl: v.tensor_tensor(u[:, k8, tsl], y2[:], yv[:], ALU.mult),
                                         reads=["y2", "yv"], writes=["u%d" % k8])

                            nconv = NTB * 8
                            assert nconv <= NP
                            cidx = 0
                            emit_bu(0)
                            for pr in range(NP):
                                if pr + 1 < NP:
                                    emit_bu(pr + 1)
                                want = (nconv * (pr + 1) + NP - 1) // NP
                                todo = []
                                while cidx < want and not todo:
                                    conv_proj(cidx)
                                    todo.append(cidx)
                                    cidx += 1
                                scan_pair(pr)
                                for ci in todo:
                                    conv_mul(ci)
                                if pr % 4 == 3:
                                    epilogue(pr // 4)
                        S.barrier()
                        S.dma("sp", ch_in.ap().rearrange("p (a b) -> p a b", b=2), chh[:, :, T:T + 2],
                              reads=["chh"], writes=["ch_in"])
                        S.cc(lambda g: g.collective_compute("AllGather", ALU.bypass, replica_groups=groups,
                                                            ins=[ch_in.ap().opt()], outs=[ch_out.ap().opt()]),
                             reads=["ch_in"], writes=["ch_out"])
                        S.dma("sp", cy_in.ap().rearrange("(k p) t -> p k t", p=128), u[:], reads=["u"], writes=["cy_in"])
                        S.dma("sp", halo[:].rearrange("p a b -> p (a b)"), ch_out.ap()[0:128, :],
                              reads=["ch_out"], writes=["halo"])
                        S.op("dve", lambda v: v.tensor_scalar(chh[:, :, 0:2], halo[:], fl, None, ALU.mult),
                             reads=["halo", "flag", "chh"], writes=["chh"])
                        S.barrier()
                        S.cc(lambda g: g.collective_compute("AllGather", ALU.bypass, replica_groups=groups,
                                                            ins=[cy_in.ap().opt()], outs=[cy_out.ap().opt()]),
                             reads=[], writes=["cy_out"])
                    with ExitStack() as st2:
                        wglu = sb("wglu", (128, 8, 128), BF16, st2)
                        wgluv = w_glu[l].rearrange("(kc p) f -> p kc f", p=128)
                        y3 = sb("y3", (128, 16, TB), BF16, st2)
                        sq = [sb("sq%d" % i, (128, TB), BF16, st2) for i in range(2)]
                        wo = [sb("wo%d" % i, (128, KC, 256), BF16, st2) for i in range(2)]
                        xres = sb("xresm", (128, 4, D), F32, st2)
                        cv = sb("cv", (128, TB), F32, st2)
                        sgm = sb("sgm", (128, TB), F32, st2)
                        yf = sb("yf", (128, TB), F32, st2)
                        rst = sb("rst", (128, 2, 4), F32, st2)
                        rsacc = sb("rsacc", (128, 2, 4), F32, st2)
                        xbm = sb("xbm", (128, D), BF16, st2)
                        wov = w_out[l].rearrange("(kc p) d -> p kc d", p=128)
                        yg = sb("yg", (128, 8, T), BF16, st2)
                        ytmp = sb("ytmp", (128, 4, T), BF16, st2)
                        def load_yg():
                            cyv = cy_out.ap().rearrange("(k p) t -> p k t", p=128)
                            S.dma("sp", yg[:], cyv[:, :, 0:T], reads=["cy_out"], writes=["yg"])
                            S.op("dve", lambda v: v.tensor_scalar(yg[:], yg[:], nfl, None, ALU.mult),
                                 reads=["yg", "flag"], writes=["yg"])
                            for hf in range(2):
                                S.dma("sp", ytmp[:], cyv[:, hf * 4:(hf + 1) * 4, T:TS], reads=["cy_out"], writes=["ytmp"])
                                S.op("dve", lambda v, hf=hf: v.scalar_tensor_tensor(
                                    yg[:, hf * 4:(hf + 1) * 4, :], ytmp[:], fl, yg[:, hf * 4:(hf + 1) * 4, :], ALU.mult, ALU.add),
                                    reads=["yg", "ytmp", "flag"], writes=["yg"])

                        def stat_acc(j, half):
                            for tt in range(4):
                                S.op("pe", lambda p, tt=tt: p.matmul(
                                    psS[:, half * 4 + tt:half * 4 + tt + 1], sq[j % 2][:, tt * 128:(tt + 1) * 128], onesb[:],
                                    start=True, stop=True),
                                    reads=["sq%d" % (j % 2), "onesb"], writes=["psS"], mark=(tt == 3))
                            if j == 0:
                                S.op("dve", lambda v: v.tensor_copy(out=rsacc[:, half, :], in_=psS[:, half * 4:half * 4 + 4]),
                                     reads=["psS"], writes=["rsacc"])
                            else:
                                S.op("dve", lambda v: v.tensor_tensor(rsacc[:, half, :], rsacc[:, half, :],
                                                                      psS[:, half * 4:half * 4 + 4], ALU.add),
                                     reads=["psS", "rsacc"], writes=["rsacc"])

                        for tb in range(NTB):
                            tsl = slice(tb * TB, (tb + 1) * TB)
                            for tt in range(4):
                                g = tb * 4 + tt
                                S.dma("sp", xres[:, tt, :], xr[g * 128:(g + 1) * 128, :],
                                      reads=["xr%d" % g], writes=["xres%d" % tt])
                            o = tb * TB
                            for j in range(8):
                                S.op("dve", lambda v, j=j: v.tensor_scalar(cv[:], chh[:, j, o:o + TB], cp(1, j), cp(4, j),
                                                                          ALU.mult, ALU.add),
                                     reads=["chh", "chp"], writes=["cv"])
                                S.op("dve", lambda v, j=j: v.scalar_tensor_tensor(cv[:], chh[:, j, o + 1:o + 1 + TB], cp(2, j), cv[:],
                                                                                 ALU.mult, ALU.add),
                                     reads=["chh", "cv", "chp"], writes=["cv"])
                                S.op("dve", lambda v, j=j: v.scalar_tensor_tensor(cv[:], chh[:, j, o + 2:o + 2 + TB], cp(3, j), cv[:],
                                                                                 ALU.mult, ALU.add),
                                     reads=["chh", "cv", "chp"], writes=["cv"])
                                S.op("dve", lambda v, j=j: v.tensor_tensor(cv[:], cv[:], gbs[:, j, tsl], ALU.mult),
                                     reads=["cv", "gbs"], writes=["cv"])
                                S.op("act", lambda a, j=j: a.activation(out=sq[j % 2][:], in_=cv[:], func=AF.Square),
                                     reads=["cv"], writes=["sq%d" % (j % 2)])
                                stat_acc(j, 1)
                                S.op("dve", lambda v, j=j: v.tensor_scalar(y3[:, 8 + j, :], cv[:], cp(6, j), None, ALU.mult),
                                     reads=["cv", "chp"], writes=["y3"])
                            if tb == 0:
                                load_yg()
                            for j in range(8):
                                S.dma("pool", wglu[:], wgluv[:, :, j * 128:(j + 1) * 128], writes=["wglu"])
                                for kc in range(8):
                                    S.op("pe", lambda p, j=j, kc=kc: p.matmul(
                                        psA[3][:], wglu[:, kc, :], yg[:, kc, tsl],
                                        start=(kc == 0), stop=(kc == 7)),
                                        reads=["wglu", "yg"], writes=["psA3"], mark=(kc == 7))
                                if j > 0:
                                    stat_acc(j - 1, 0)
                                S.op("act", lambda a: a.activation(out=sgm[:], in_=psA[3][:], func=AF.Sigmoid),
                                     reads=["psA3"], writes=["sgm"])
                                S.op("dve", lambda v, j=j: v.tensor_tensor(yf[:], sgm[:], yg[:, j, tsl], ALU.mult),
                                     reads=["sgm", "yg"], writes=["yf"])
                                S.op("act", lambda a, j=j: a.activation(out=sq[j % 2][:], in_=yf[:], func=AF.Square),
                                     reads=["yf"], writes=["sq%d" % (j % 2)])
                                S.op("dve", lambda v, j=j: v.tensor_scalar(y3[:, j, :], yf[:], cp(5, j), None, ALU.mult),
                                     reads=["yf", "chp"], writes=["y3"])
                            stat_acc(7, 0)
                            S.op("dve", lambda v: v.tensor_scalar(rst[:], rsacc[:], 1.0 / 1024.0, RMS_EPS, ALU.mult, ALU.add),
                                 reads=["rsacc"], writes=["rst"])
                            S.op("act", lambda a: a.activation(out=rst[:], in_=rst[:], func=AF.Sqrt),
                                 reads=["rst"], writes=["rst"])
                            S.op("dve", lambda v: v.reciprocal(rst[:], rst[:]), reads=["rst"], writes=["rst"])
                            S.op("dve", lambda v: v.tensor_scalar(rst[:], rst[:], 1.0 / ALPHA, None, ALU.mult),
                                 reads=["rst"], writes=["rst"])
                            stg = (sgm, yf)
                            stk = ("sgm", "yf")
                            for dt in range(D // 128):
                                wb = (dt // 2) % 2
                                sub = dt % 2
                                if sub == 0:
                                    c0 = (dt // 2) * 256
                                    S.dma("pool", wo[wb][:], wov[:, :, c0:c0 + 256], writes=["wo%d" % wb])
                                for half in range(2):
                                    for kc in range(8):
                                        S.op("pe", lambda p, half=half, kc=kc, wb=wb, sub=sub: p.matmul(
                                            psB[half][:], wo[wb][:, half * 8 + kc, sub * 128:(sub + 1) * 128],
                                            y3[:, half * 8 + kc, :], start=(kc == 0), stop=(kc == 7)),
                                            reads=["y3", "wo%d" % wb], writes=["psB%d" % half], mark=(kc == 7))
                                    S.op("act", lambda a, half=half: a.activation(out=stg[half][:], in_=psB[half][:], func=AF.Copy),
                                         reads=["psB%d" % half], writes=[stk[half]])
                                    for tt in range(4):
                                        S.op("pe", lambda p, tt=tt, half=half: p.transpose(
                                            psA[half][:, tt * 128:(tt + 1) * 128], stg[half][:, tt * 128:(tt + 1) * 128], identf[:]),
                                            reads=[stk[half], "identf"], writes=["psA%d" % half], mark=(tt == 3))
                                for tt in range(4):
                                    xs = xres[:, tt, dt * 128:(dt + 1) * 128]
                                    for half in range(2):
                                        S.op("dve", lambda v, tt=tt, xs=xs, half=half: v.scalar_tensor_tensor(
                                            xs, psA[half][:, tt * 128:(tt + 1) * 128], rst[:, half, tt:tt + 1], xs, ALU.mult, ALU.add),
                                            reads=["psA%d" % half, "rst", "xres%d" % tt], writes=["xres%d" % tt])
                            for tt in range(4):
                                layer_norm_tile(xres[:, tt, :], "xres%d" % tt, tb * 4 + tt,
                                                LN_EPS / (ALPHA * ALPHA), xr, xbm[:], ["xb"])
                S.barrier()

            for l in range(L):
                src = x_in if l == 0 else xr
                ffn_stage(l, w_f1g, w_f1u, w_f1d, 0, src, xr)
                if stop_after == "ffn1":
                    break
                mixer_stage(l)
                if stop_after == "mixer":
                    break
                ffn_stage(l, w_f2g, w_f2u, w_f2d, 2, xr, y_out if l == L - 1 else xr)
            if stop_after is not None:
                S.barrier()
                S.dma("sp", y_out, xr)
            S.barrier()
    return nc


_NC = None


def _prep(half, x, ffn1_gate, ffn1_up, ffn1_down, ln1_g, ln1_b, w_in, s5_lam_re, s5_lam_im, s5_log_dt,
          s5_b_re, s5_b_im, s5_c_re, s5_c_im, s5_d, s5_w_glu, conv_w, conv_b, g_s5, g_conv, w_out,
          ln2_g, ln2_b, ffn2_gate, ffn2_up, ffn2_down, ln3_g, ln3_b):
    f = lambda a: np.ascontiguousarray(np.asarray(a, dtype=np.float32))
    L = np.asarray(ffn1_gate).shape[0]
    lnp = np.stack([f(ln1_g), f(ln1_b), f(ln2_g), f(ln2_b), f(ln3_g), f(ln3_b)], axis=1)
    w_in = f(w_in)
    cw = f(conv_w)
    per_ch = np.stack([f(s5_d).reshape(L, 1024), cw[:, 0], cw[:, 1], cw[:, 2], f(conv_b), f(g_s5), f(g_conv)],
                      axis=1)
    chp = f(per_ch.reshape(L, 7, 8, 128).transpose(3, 0, 1, 2))
    shared = dict(ffn1_gate=f(ffn1_gate), ffn1_up=f(ffn1_up), ffn1_down=f(ffn1_down),
                  ffn2_gate=f(ffn2_gate), ffn2_up=f(ffn2_up), ffn2_down=f(ffn2_down),
                  w_in=w_in, s5_w_glu=f(s5_w_glu), w_out=f(w_out), lnp=lnp, chp=chp,
                  ident=np.eye(128, dtype=np.float32))
    lre_a = f(s5_lam_re).reshape(L, 32, 2, 64)
    lim_a = f(s5_lam_im).reshape(L, 32, 2, 64)
    ldt_a = np.broadcast_to(f(s5_log_dt).reshape(L, 32, 2, 1), (L, 32, 2, 64))
    b_re = f(s5_b_re).reshape(L, 32, 2, 64, 16); b_im = f(s5_b_im).reshape(L, 32, 2, 64, 16)
    c_re = f(s5_c_re).reshape(L, 32, 2, 16, 64); c_im = f(s5_c_im).reshape(L, 32, 2, 16, 64)
    dd = f(s5_d).reshape(L, 8, 128)
    per = []
    for h in range(2):
        ps_ = slice(16 * h, 16 * h + 16)
        s5s = np.stack([lre_a[:, ps_], lim_a[:, ps_], ldt_a[:, ps_]], axis=0)
        s5s = f(s5s.transpose(3, 4, 1, 0, 2).reshape(128, L, 3, 16))

        def pad(arr, tr):
            out = np.zeros((L, 128, 16, 128), np.float32)
            for pr in range(16):
                for gg in range(2):
                    g8 = (2 * pr + gg) % 8
                    blk = arr[:, 16 * h + pr, gg]
                    if tr:
                        blk = blk.transpose(0, 2, 1)
                    out[:, gg * 64:(gg + 1) * 64, pr, g8 * 16:(g8 + 1) * 16] = blk
            return out
        s5b = np.stack([pad(b_re, False), pad(b_im, False)], axis=1)
        s5c = np.stack([pad(c_re, True), pad(c_im, True)], axis=1)
        dloc = f(dd[:, 4 * h:4 * h + 4].transpose(2, 0, 1))
        wu = w_in[:, :, :1024]
        w_in_u = f(np.concatenate([wu[:, :, 512 * h:512 * h + 512], wu[:, :, 512 * (1 - h):512 * (1 - h) + 512]], axis=2))
        flag = np.zeros((128, 2), np.float32); flag[:, 0] = h; flag[:, 1] = 1 - h
        per.append(dict(s5s=s5s, s5b=s5b, s5c=s5c, dloc=dloc, w_in_u=w_in_u, flag=flag))
    return shared, per


def kernel(**inputs):
    global _NC
    x = np.ascontiguousarray(np.asarray(inputs["x"], dtype=np.float32))
    shared, per = _prep(None, **inputs)
    if _NC is None:
        _NC = build_nc(L=4, T=1024, FF=5632)
    in_maps = []
    for c in range(NCORES):
        b, h = c // 2, c % 2
        in_maps.append(dict(shared, x=np.ascontiguousarray(x[b, h * 1024:(h + 1) * 1024]), **per[h]))
    res = run_bass_kernel_spmd(_NC, in_maps, core_ids=list(range(NCORES)))
    out = np.empty((4, 2048, 2048), np.float32)
    for c in range(NCORES):
        b, h = c // 2, c % 2
        out[b, h * 1024:(h + 1) * 1024] = np.asarray(res.results[c]["y"], dtype=np.float32)
    return out
```

```python
import math
from contextlib import ExitStack
import numpy as np
import concourse.bass as bass
import concourse.mybir as mybir
from concourse.bass_utils import run_bass_kernel_spmd

F32 = mybir.dt.float32
BF16 = mybir.dt.bfloat16
AF = mybir.ActivationFunctionType
ALU = mybir.AluOpType

D = 2048
KC = D // 128
TB = 512
DIN = 4096
ALPHA = (2.0 * 4) ** 0.25
LN_EPS = 1e-5
RMS_EPS = 1e-6
NCORES = 8
PAIR_GROUPS = [[0, 1], [2, 3], [4, 5], [6, 7]]


class Tok:
    __slots__ = ("sem", "val", "eng")

    def __init__(self, sem, val, eng):
        self.sem, self.val, self.eng = sem, val, eng


class Sched:
    def __init__(self, nc, es):
        self.nc = nc
        self.e = {}
        for name, obj in (("pe", nc.tensor), ("act", nc.scalar), ("dve", nc.vector),
                          ("pool", nc.gpsimd), ("sp", nc.sync)):
            sem = es.enter_context(nc.semaphore("s_" + name))
            self.e[name] = dict(obj=obj, sem=sem, cnt=0, waited={})
        self.nds = 6
        self.dsem = {q: [es.enter_context(nc.semaphore("d_%s%d" % (q, i))) for i in range(self.nds)]
                     for q in ("sp", "pool")}
        self.dcnt = {q: [0] * self.nds for q in ("sp", "pool")}
        self.drr = {"sp": 0, "pool": 0}
        self.lastw = {}
        self.readers = {}
        self.ccsem = es.enter_context(nc.semaphore("s_cc"))

    def _wait(self, en, tok):
        if tok is None:
            return
        if getattr(self, "skip_eng", None) is not None and tok.eng == self.skip_eng:
            return
        e = self.e[en]
        k = id(tok.sem)
        if e["waited"].get(k, 0) >= tok.val:
            return
        e["obj"].wait_ge(tok.sem, tok.val)
        e["waited"][k] = tok.val

    def _deps(self, en, reads, writes):
        for k in reads:
            self._wait(en, self.lastw.get(k))
        for k in writes:
            self._wait(en, self.lastw.get(k))
            for t in self.readers.get(k, {}).values():
                self._wait(en, t)

    def _record(self, tok, reads, writes, rk):
        for k in reads:
            self.readers.setdefault(k, {})[rk] = tok
        for k in writes:
            self.lastw[k] = tok
            self.readers[k] = {}

    def op(self, en, fn, reads=(), writes=(), mark=True, nosame=False):
        self.skip_eng = en if nosame else None
        self._deps(en, reads, writes)
        self.skip_eng = None
        ins = fn(self.e[en]["obj"])
        if mark:
            e = self.e[en]
            e["cnt"] += 1
            ins.then_inc(e["sem"], 1)
            tok = Tok(e["sem"], e["cnt"], en)
            self._record(tok, reads, writes, en)
        return ins

    def dma(self, q, out, in_, reads=(), writes=()):
        self._deps(q, reads, writes)
        j = self.drr[q]
        self.drr[q] = (j + 1) % self.nds
        sem = self.dsem[q][j]
        if self.dcnt[q][j] > 0:
            self._wait(q, Tok(sem, self.dcnt[q][j], q))
        self.e[q]["obj"].dma_start(out=out, in_=in_).then_inc(sem, 16)
        self.dcnt[q][j] += 16
        tok = Tok(sem, self.dcnt[q][j], q)
        self._record(tok, reads, writes, "%s%d" % (q, j))
        return tok

    def cc(self, fn, reads=(), writes=()):
        self._deps("pool", reads, writes)
        self.ccn = getattr(self, "ccn", 0) + 1
        fn(self.e["pool"]["obj"]).then_inc(self.ccsem)
        tok = Tok(self.ccsem, self.ccn, "pool")
        self._record(tok, reads, writes, "cc")
        return tok

    def barrier(self):
        toks = []
        for en, e in self.e.items():
            if e["cnt"] > 0:
                toks.append(Tok(e["sem"], e["cnt"], en))
        for q in ("sp", "pool"):
            for j in range(self.nds):
                if self.dcnt[q][j] > 0:
                    toks.append(Tok(self.dsem[q][j], self.dcnt[q][j], q))
        if getattr(self, "ccn", 0) > 0:
            toks.append(Tok(self.ccsem, self.ccn, "pool"))
        for en in self.e:
            for t in toks:
                self._wait(en, t)
        self.lastw = {}
        self.readers = {}


def build_nc(L=4, T=1024, FF=5632, stop_after=None, debug=False, groups=None):
    nc = bass.Bass("TRN2", target_bir_lowering=False)
    groups = groups or PAIR_GROUPS
    NFT = FF // 128
    NTB = T // TB
    TS = 2 * T
    NSB = TS // TB
    NLEV = int(round(math.log2(TS)))
    NP = 16

    def din(name, shape):
        return nc.dram_tensor(name, list(shape), F32, kind="ExternalInput").ap()

    x_in = din("x", (T, D))
    w_f1g = din("ffn1_gate", (L, D, FF)); w_f1u = din("ffn1_up", (L, D, FF)); w_f1d = din("ffn1_down", (L, FF, D))
    w_f2g = din("ffn2_gate", (L, D, FF)); w_f2u = din("ffn2_up", (L, D, FF)); w_f2d = din("ffn2_down", (L, FF, D))
    w_in_u = din("w_in_u", (L, D, 1024))
    flag_in = din("flag", (128, 2))
    dloc_in = din("dloc", (128, L, 4))
    w_in = din("w_in", (L, D, DIN)); w_glu = din("s5_w_glu", (L, 1024, 1024)); w_out = din("w_out", (L, D, D))
    lnp = din("lnp", (L, 6, D))
    s5s = din("s5s", (128, L, 3, 16))
    s5b = din("s5b", (L, 2, 128, 16, 128))
    s5c = din("s5c", (L, 2, 128, 16, 128))
    chp = din("chp", (128, L, 7, 8))
    ident_in = din("ident", (128, 128))
    y_out = nc.dram_tensor("y", [T, D], F32, kind="ExternalOutput").ap()
    xr = nc.dram_tensor("xr", [T, D], F32).ap()
    if debug:
        dbg_u = nc.dram_tensor("dbg_u", [128, 4, TS], BF16, kind="ExternalOutput").ap()
        dbg_ys = nc.dram_tensor("dbg_ys", [128, 4, TS], BF16, kind="ExternalOutput").ap()
        dbg_yg = nc.dram_tensor("dbg_yg", [128, 8, T], BF16, kind="ExternalOutput").ap()
        dbg_y3 = nc.dram_tensor("dbg_y3", [128, 16, TB], BF16, kind="ExternalOutput").ap()
        dbg_rst = nc.dram_tensor("dbg_rst", [128, 2, 4], F32, kind="ExternalOutput").ap()
        dbg_halo = nc.dram_tensor("dbg_halo", [128, 8, 2], F32, kind="ExternalOutput").ap()
    cu_in = nc.dram_tensor("cu_in", [1024, T], BF16)
    cu_out = nc.dram_tensor("cu_out", [2048, T], BF16)
    cy_in = nc.dram_tensor("cy_in", [512, TS], BF16)
    cy_out = nc.dram_tensor("cy_out", [1024, TS], BF16)
    ch_in = nc.dram_tensor("ch_in", [128, 16], BF16)
    ch_out = nc.dram_tensor("ch_out", [256, 16], BF16)

    es = ExitStack()
    with es:
        S = Sched(nc, es)

        uniq = [0]

        def sb(name, shape, dt, stack=es):
            uniq[0] += 1
            return stack.enter_context(nc.sbuf_tensor("%s_%d" % (name, uniq[0]), list(shape), dt))

        def ps(name, shape, dt=F32, stack=es):
            return stack.enter_context(nc.psum_tensor(name, list(shape), dt))

        block = es.enter_context(nc.Block())

        @block.gpsimd
        def _(_g):
            xT = sb("xT", (128, KC, T), BF16)
            identb = sb("identb", (128, 128), BF16)
            identf = sb("identf", (128, 128), F32)
            onesb = sb("onesb", (128, 1), BF16)
            chp_sb = sb("chp_sb", (128, L, 7, 8), F32)
            s5s_sb = sb("s5s_sb", (128, L, 3, 16), F32)
            flag_sb = sb("flag_sb", (128, 2), F32)
            dloc_sb = sb("dloc_sb", (128, L, 4), F32)
            gb_sb = sb("gb_sb", (128, 2, D), F32)
            lnst = sb("lnst", (128, 4, 6), F32)
            lnmv = sb("lnmv", (128, 4), F32)
            pst = ps("pst", (128, 1024), BF16)
            psA = [ps("psA%d" % i, (128, 512)) for i in range(4)]
            psB = [ps("psB%d" % i, (128, 512)) for i in range(2)]
            psS = ps("psS", (128, 512))

            S.dma("sp", identf[:], ident_in, writes=["identf"])
            S.dma("pool", identb[:], ident_in, writes=["identb"])
            S.dma("sp", chp_sb[:], chp, writes=["chp"])
            S.dma("sp", s5s_sb[:], s5s, writes=["s5s"])
            S.dma("sp", flag_sb[:], flag_in, writes=["flag"])
            S.dma("sp", dloc_sb[:], dloc_in, writes=["dloc"])
            S.op("dve", lambda v: v.memset(onesb[:], 1.0), writes=["onesb"])

            def load_gb(l, which):
                for i in range(2):
                    S.dma("sp", gb_sb[:, i, :], lnp[l, 2 * which + i, :].partition_broadcast(128),
                          writes=["gb"])

            def to_xT(src, srck, tt, xb, xbk):
                S.op("act", lambda a: a.activation(out=xb, in_=src, func=AF.Copy),
                     reads=[srck], writes=xbk)
                for half in range(2):
                    for j in range(8):
                        dc = half * 8 + j
                        S.op("pe", lambda p, dc=dc, j=j: p.transpose(pst[:, j * 128:(j + 1) * 128],
                                                                   xb[:, dc * 128:(dc + 1) * 128], identb[:]),
                             reads=xbk + ["identb"], writes=["pst"], mark=(j == 7))
                    S.op("act", lambda a, half=half: a.activation(
                        out=xT[:, half * 8:(half + 1) * 8, tt * 128:(tt + 1) * 128],
                        in_=pst[:].rearrange("p (c t) -> p c t", c=8), func=AF.Copy),
                        reads=["pst"], writes=["xT%d" % (tt // 4)])

            def layer_norm_tile(v, vk, tt, eps, dst, xb, xbk):
                for c in range(4):
                    S.op("dve", lambda e, c=c: e.bn_stats(lnst[:, c, :], v[:, c * 512:(c + 1) * 512]),
                         reads=[vk], writes=["lnst"], mark=(c == 3))
                S.op("dve", lambda e: e.bn_aggr(lnmv[:, 0:2], lnst[:].rearrange("p a b -> p (a b)")),
                     reads=["lnst"], writes=["lnmv"])
                S.op("dve", lambda e: e.tensor_scalar(lnmv[:, 2:3], lnmv[:, 1:2], float(eps), None, ALU.add),
                     reads=["lnmv"], writes=["lnmv"])
                S.op("act", lambda a: a.activation(out=lnmv[:, 2:3], in_=lnmv[:, 2:3], func=AF.Sqrt),
                     reads=["lnmv"], writes=["lnmv"])
                S.op("dve", lambda e: e.reciprocal(lnmv[:, 2:3], lnmv[:, 2:3]),
                     reads=["lnmv"], writes=["lnmv"])
                S.op("dve", lambda e: e.tensor_scalar(v, v, lnmv[:, 0:1], lnmv[:, 2:3],
                                                      ALU.subtract, ALU.mult),
                     reads=[vk, "lnmv"], writes=[vk])
                S.op("dve", lambda e: e.tensor_tensor(v, v, gb_sb[:, 0, :], ALU.mult),
                     reads=[vk, "gb"], writes=[vk])
                S.op("dve", lambda e: e.tensor_tensor(v, v, gb_sb[:, 1, :], ALU.add),
                     reads=[vk, "gb"], writes=[vk])
                S.dma("sp", dst[tt * 128:(tt + 1) * 128, :], v, reads=[vk], writes=["xr%d" % tt])
                to_xT(v, vk, tt, xb, xbk)

            with ExitStack() as st:
                x0 = [sb("x0_%d" % i, (128, D), F32, st) for i in range(2)]
                xb0 = sb("xb0", (128, D), BF16, st)
                for tt in range(T // 128):
                    b = tt % 2
                    S.dma("sp", x0[b][:], x_in[tt * 128:(tt + 1) * 128, :], writes=["x0_%d" % b])
                    to_xT(x0[b][:], "x0_%d" % b, tt, xb0[:], ["xb"])
            S.barrier()

            def ffn_stage(l, wg, wu, wd, lnidx, src, dst):
                load_gb(l, lnidx)
                c1 = 0.5 / ALPHA
                eps = LN_EPS / (ALPHA * ALPHA)
                with ExitStack() as st:
                    hT = sb("hT", (128, NFT, TB), BF16, st)
                    R = sb("R", (128, 4, 2 * KC * 256), BF16, st)
                    wgu = [R[:, i, :].rearrange("p (w k c) -> p w k c", w=2, k=KC) for i in range(4)]
                    wdb = [R[:, 2 * i:2 * i + 2, :].rearrange("p a b -> p (a b)")[:, 0:NFT * 256]
                           .rearrange("p (f c) -> p f c", c=256) for i in range(2)]
                    wdk = [["R0", "R1"], ["R2", "R3"]]
                    xbf = sb("xbf", (128, D), BF16, st)
                    xres = sb("xres", (128, 4, D), F32, st)
                    sgx = sb("sgx", (128, 2 * TB), F32, st)
                    sg = [sgx[:, 0:TB], sgx[:, TB:2 * TB]]
                    wgv = wg[l].rearrange("(kc p) f -> p kc f", p=128)
                    wuv = wu[l].rearrange("(kc p) f -> p kc f", p=128)
                    wdv = wd[l].rearrange("(fc p) d -> p fc d", p=128)
                    def loadx(tb):
                        for tt in range(4):
                            g = tb * 4 + tt
                            S.dma("sp", xres[:, tt, :], src[g * 128:(g + 1) * 128, :],
                                  reads=["xr%d" % g], writes=["xres%d" % tt])

                    def up(tb):
                        tsl = slice(tb * TB, (tb + 1) * TB)
                        for ft in range(NFT):
                            wb = (ft // 2) % 4
                            sub = ft % 2
                            b = ft % 2
                            if sub == 0:
                                c0 = (ft // 2) * 256
                                S.dma("pool", wgu[wb][:, 0], wgv[:, :, c0:c0 + 256], writes=["R%d" % wb])
                                S.dma("pool", wgu[wb][:, 1], wuv[:, :, c0:c0 + 256], writes=["R%d" % wb])
                            for which in range(2):
                                pt = psA[2 * b + which]
                                for kc in range(KC):
                                    S.op("pe", lambda p, pt=pt, which=which, kc=kc, wb=wb, sub=sub: p.matmul(
                                        pt[:], wgu[wb][:, which, kc, sub * 128:(sub + 1) * 128], xT[:, kc, tsl],
                                        start=(kc == 0), stop=(kc == KC - 1)),
                                        reads=["R%d" % wb, "xT%d" % tb], writes=["psA%d" % (2 * b + which)],
                                        mark=(kc == KC - 1))
                            S.op("act", lambda a, b=b: a.activation(out=sg[b][:], in_=psA[2 * b][:], func=AF.Silu),
                                 reads=["psA%d" % (2 * b)], writes=["sg%d" % b])
                            S.op("dve", lambda v, b=b, ft=ft: v.tensor_tensor(hT[:, ft, :], sg[b][:],
                                                                              psA[2 * b + 1][:], ALU.mult),
                                 reads=["sg%d" % b, "psA%d" % (2 * b + 1)], writes=["hT"])
                    def down(tb):
                        for dt in range(D // 128):
                            b = dt % 2
                            wb = (dt // 2) % 2
                            sub = dt % 2
                            if sub == 0:
                                c0 = (dt // 2) * 256
                                S.dma("pool", wdb[wb], wdv[:, :, c0:c0 + 256], writes=wdk[wb])
                            for fc in range(NFT):
                                S.op("pe", lambda p, fc=fc, b=b, wb=wb, sub=sub: p.matmul(
                                    psB[b][:], wdb[wb][:, fc, sub * 128:(sub + 1) * 128], hT[:, fc, :],
                                    start=(fc == 0), stop=(fc == NFT - 1)),
                                    reads=["hT"] + wdk[wb], writes=["psB%d" % b], mark=(fc == NFT - 1))
                            S.op("act", lambda a, b=b: a.activation(out=sg[b][:], in_=psB[b][:], func=AF.Copy, scale=float(c1)),
                                 reads=["psB%d" % b], writes=["sg%d" % b])
                            if dt > 0:
                                back(dt - 1)
                        back(D // 128 - 1)

                    def back(dt):
                        b = dt % 2
                        for tt in range(4):
                            S.op("pe", lambda p, tt=tt, b=b: p.transpose(
                                psA[b][:, tt * 128:(tt + 1) * 128], sg[b][:, tt * 128:(tt + 1) * 128], identf[:]),
                                reads=["sg%d" % b, "identf"], writes=["psA%d" % b], mark=(tt == 3))
                        S.op("dve", lambda v, dt=dt, b=b: v.tensor_tensor(
                            xres[:, :, dt * 128:(dt + 1) * 128], xres[:, :, dt * 128:(dt + 1) * 128],
                            psA[b][:].rearrange("p (t d) -> p t d", t=4), ALU.add),
                            reads=["psA%d" % b] + ["xres%d" % t_ for t_ in range(4)],
                            writes=["xres%d" % t_ for t_ in range(4)])
                    def lnorm(tb):
                        for tt in range(4):
                            layer_norm_tile(xres[:, tt, :], "xres%d" % tt, tb * 4 + tt, eps, dst, xbf[:], ["xbf"])

                    for tb in range(NTB):
                        up(tb)
                        if tb > 0:
                            lnorm(tb - 1)
                        loadx(tb)
                        down(tb)
                    lnorm(NTB - 1)
                S.barrier()

            def mixer_stage(l):
                load_gb(l, 1)
                w_in_v = w_in[l].rearrange("(kc p) f -> p kc f", p=128)
                w_in_uv = w_in_u[l].rearrange("(kc p) f -> p kc f", p=128)
                cp = lambda which, j: chp_sb[:, l, which, j:j + 1]
                fl = flag_sb[:, 0:1]
                nfl = flag_sb[:, 1:2]
                with ExitStack() as st:
                    chh = sb("chh", (128, 8, 2 + T), BF16, st)
                    gbs = sb("gbs", (128, 8, T), BF16, st)
                    halo = sb("halo", (128, 8, 2), BF16, st)
                    with ExitStack() as stU:
                        u = sb("u", (128, 4, TS), BF16, stU)
                        coef = sb("coef", (128, NP, NLEV, 3), F32, stU)
                        prm = sb("prm", (128, 12, NP), F32, stU)
                        lhsB = sb("lhsB", (128, NP, 2, 128), BF16, stU)
                        lhsC = sb("lhsC", (128, NP, 2, 128), BF16, stU)
                        with ExitStack() as st2:
                            wt = [sb("wt%d" % i, (128, KC, 128), BF16, st2) for i in range(2)]
                            ust = sb("ust", (128, 8, T), BF16, st2)
                            ua = sb("ua", (128, 4, T), BF16, st2)
                            ub = sb("ub", (128, 4, T), BF16, st2)
                            for j in range(8):
                                b = j % 2
                                S.dma("pool", wt[b][:], w_in_uv[:, :, j * 128:(j + 1) * 128], writes=["wt%d" % b])
                                for tb in range(NTB):
                                    pt = psA[tb]
                                    for kc in range(KC):
                                        S.op("pe", lambda p, pt=pt, kc=kc, b=b, tb=tb: p.matmul(
                                            pt[:], wt[b][:, kc, :], xT[:, kc, tb * TB:(tb + 1) * TB],
                                            start=(kc == 0), stop=(kc == KC - 1)),
                                            reads=["wt%d" % b, "xT%d" % tb], writes=["psA%d" % tb], mark=(kc == KC - 1))
                                    S.op("act", lambda a, pt=pt, j=j, tb=tb: a.activation(
                                        out=ust[:, j, tb * TB:(tb + 1) * TB], in_=pt[:], func=AF.Copy),
                                        reads=["psA%d" % tb], writes=["ust"])
                            S.dma("sp", cu_in.ap().rearrange("(k p) t -> p k t", p=128), ust[:],
                                  reads=["ust"], writes=["cu_in"])
                            S.cc(lambda g: g.collective_compute("AllGather", ALU.bypass, replica_groups=groups,
                                                                ins=[cu_in.ap().opt()], outs=[cu_out.ap().opt()]),
                                 reads=["cu_in"], writes=["cu_out"])
                            lre = s5s_sb[:, l, 0, :]; lim = s5s_sb[:, l, 1, :]; ldt = s5s_sb[:, l, 2, :]
                            P = lambda i: prm[:, i, :]
                            V = lambda fn, r=("prm", "s5s"), w=("prm",): S.op("dve", fn, reads=list(r), writes=list(w))
                            S.op("act", lambda a: a.activation(out=P(0), in_=ldt, func=AF.Exp),
                                 reads=["s5s", "prm"], writes=["prm"])
                            V(lambda v: v.tensor_tensor(P(1), lre, P(0), ALU.mult))
                            V(lambda v: v.tensor_tensor(P(2), lim, P(0), ALU.mult))
                            S.op("act", lambda a: a.activation(out=P(3), in_=P(1), func=AF.Exp),
                                 reads=["prm"], writes=["prm"])
                            V(lambda v: v.tensor_scalar(P(5), P(2), 1.0 / 32.0, 0.5 * math.pi, ALU.mult, ALU.add))
                            S.op("act", lambda a: a.activation(out=P(4), in_=P(2), func=AF.Sin, scale=1.0 / 32.0),
                                 reads=["prm"], writes=["prm"])
                            S.op("act", lambda a: a.activation(out=P(5), in_=P(5), func=AF.Sin),
                                 reads=["prm"], writes=["prm"])

                            def csq(cr, ci):
                                V(lambda v: v.tensor_tensor(P(7), cr, cr, ALU.mult), r=("prm", "coef"), w=("prm", "coef"))
                                V(lambda v: v.tensor_tensor(P(8), ci, ci, ALU.mult), r=("prm", "coef"), w=("prm", "coef"))
                                V(lambda v: v.tensor_tensor(P(7), P(7), P(8), ALU.subtract), r=("prm", "coef"), w=("prm", "coef"))
                                V(lambda v: v.tensor_tensor(P(8), cr, ci, ALU.mult), r=("prm", "coef"), w=("prm", "coef"))
                                return P(7), P(8)
                            for _i in range(5):
                                re_, imh = csq(P(5), P(4))
                                V(lambda v: v.tensor_copy(out=P(5), in_=re_))
                                V(lambda v: v.tensor_scalar(P(4), imh, 2.0, None, ALU.mult))
                            V(lambda v: v.tensor_tensor(coef[:, :, 0, 0], P(3), P(5), ALU.mult), w=("prm", "coef"))
                            V(lambda v: v.tensor_tensor(coef[:, :, 0, 1], P(3), P(4), ALU.mult), w=("prm", "coef"))
                            for k in range(1, NLEV):
                                re_, imh = csq(coef[:, :, k - 1, 0], coef[:, :, k - 1, 1])
                                V(lambda v, k=k: v.tensor_copy(out=coef[:, :, k, 0], in_=re_), r=("prm", "coef"), w=("prm", "coef"))
                                V(lambda v, k=k: v.tensor_scalar(coef[:, :, k, 1], imh, 2.0, None, ALU.mult),
                                  r=("prm", "coef"), w=("prm", "coef"))
                            for k in range(NLEV):
                                V(lambda v, k=k: v.tensor_scalar(coef[:, :, k, 2], coef[:, :, k, 1], -1.0, None, ALU.mult),
                                  r=("prm", "coef"), w=("prm", "coef"))
                            V(lambda v: v.tensor_scalar(P(3), coef[:, :, 0, 0], -1.0, None, ALU.add), r=("prm", "coef"))
                            V(lambda v: v.tensor_tensor(P(6), lre, lre, ALU.mult))
                            V(lambda v: v.tensor_tensor(P(7), lim, lim, ALU.mult))
                            V(lambda v: v.tensor_tensor(P(6), P(6), P(7), ALU.add))
                            V(lambda v: v.reciprocal(P(6), P(6)))
                            V(lambda v: v.tensor_tensor(P(7), P(3), lre, ALU.mult))
                            V(lambda v: v.tensor_tensor(P(8), coef[:, :, 0, 1], lim, ALU.mult), r=("prm", "coef", "s5s"))
                            V(lambda v: v.tensor_tensor(P(7), P(7), P(8), ALU.add))
                            V(lambda v: v.tensor_tensor(P(9), P(7), P(6), ALU.mult))
                            V(lambda v: v.tensor_tensor(P(7), coef[:, :, 0, 1], lre, ALU.mult), r=("prm", "coef", "s5s"))
                            V(lambda v: v.tensor_tensor(P(8), P(3), lim, ALU.mult))
                            V(lambda v: v.tensor_tensor(P(7), P(7), P(8), ALU.subtract))
                            V(lambda v: v.tensor_tensor(P(10), P(7), P(6), ALU.mult))
                            with ExitStack() as st3:
                                bre = sb("bre", (128, 8, 128), F32, st3)
                                bim = sb("bim", (128, 8, 128), F32, st3)
                                t1 = sb("t1", (128, 8, 128), F32, st3)
                                t2 = sb("t2", (128, 8, 128), F32, st3)
                                for c8 in range(NP // 8):
                                    psl = slice(c8 * 8, (c8 + 1) * 8)
                                    S.dma("sp", bre[:], s5b[l, 0, :, psl, :], writes=["bre"])
                                    S.dma("sp", bim[:], s5b[l, 1, :, psl, :], writes=["bim"])
                                    qre = prm[:, 9, psl].unsqueeze(2).to_broadcast([128, 8, 128])
                                    qim = prm[:, 10, psl].unsqueeze(2).to_broadcast([128, 8, 128])
                                    W = lambda fn, r, w: S.op("dve", fn, reads=list(r) + ["prm"], writes=list(w))
                                    W(lambda v: v.tensor_tensor(t1[:], bre[:], qre, ALU.mult), ["bre"], ["t1"])
                                    W(lambda v: v.tensor_tensor(t2[:], bim[:], qim, ALU.mult), ["bim"], ["t2"])
                                    W(lambda v: v.tensor_tensor(t1[:], t1[:], t2[:], ALU.subtract), ["t1", "t2"], ["t1"])
                                    W(lambda v: v.tensor_tensor(t2[:], bim[:], qre, ALU.mult), ["bim", "t1"], ["t2"])
                                    W(lambda v: v.tensor_tensor(bim[:], bre[:], qim, ALU.mult), ["bre"], ["bim"])
                                    W(lambda v: v.tensor_tensor(t2[:], t2[:], bim[:], ALU.add), ["t2", "bim"], ["t2"])
                                    for pl, src_ in ((0, t1), (1, t2)):
                                        for g4 in range(2):
                                            for i in range(4):
                                                pp = g4 * 4 + i
                                                S.op("pe", lambda p, src_=src_, pp=pp, i=i: p.transpose(
                                                    psS[:, i * 128:(i + 1) * 128], src_[:, pp, :], identf[:]),
                                                    reads=["t1", "t2", "identf"], writes=["psS"], mark=(i == 3))
                                            S.op("act", lambda a, pl=pl, g4=g4, c8=c8: a.activation(
                                                out=lhsB[:, c8 * 8 + g4 * 4:c8 * 8 + g4 * 4 + 4, pl, :],
                                                in_=psS[:].rearrange("p (a b) -> p a b", a=4), func=AF.Copy),
                                                reads=["psS"], writes=["lhsB"])
                                    S.dma("sp", bre[:], s5c[l, 0, :, psl, :], reads=[], writes=["bre"])
                                    S.dma("sp", bim[:], s5c[l, 1, :, psl, :], reads=[], writes=["bim"])
                                    S.op("act", lambda a, psl=psl: a.activation(out=lhsC[:, psl, 0, :], in_=bre[:], func=AF.Copy),
                                         reads=["bre"], writes=["lhsC"])
                                    S.op("act", lambda a, psl=psl: a.activation(out=lhsC[:, psl, 1, :], in_=bim[:], func=AF.Copy,
                                                                              scale=-1.0),
                                         reads=["bim"], writes=["lhsC"])
                            cuv = cu_out.ap().rearrange("(r k p) t -> r p k t", r=2, p=128)
                            for r in range(2):
                                S.dma("sp", ua[:], cuv[r, :, 0:4, :], reads=["cu_out"], writes=["ua"])
                                S.dma("sp", ub[:], cuv[r, :, 4:8, :], reads=["cu_out"], writes=["ub"])
                                fa, fb = (nfl, fl) if r == 0 else (fl, nfl)
                                S.op("dve", lambda v, fa=fa: v.tensor_scalar(ua[:], ua[:], fa, None, ALU.mult),
                                     reads=["ua", "flag"], writes=["ua"])
                                S.op("dve", lambda v, fb=fb, r=r: v.scalar_tensor_tensor(
                                    u[:, :, r * T:(r + 1) * T], ub[:], fb, ua[:], ALU.mult, ALU.add),
                                    reads=["ua", "ub", "flag"], writes=["u%d" % k_ for k_ in range(4)])
                        S.barrier()
                        with ExitStack() as st2:
                            S.barrier()
                            Zr2 = [sb("Zr%d" % i, (128, TS), F32, st2) for i in range(2)]
                            Zi2 = [sb("Zi%d" % i, (128, TS), F32, st2) for i in range(2)]
                            sbf2 = [sb("sbf%d" % i, (128, 2, TS), BF16, st2) for i in range(2)]
                            yv = sb("yv", (128, TB), F32, st2)
                            y2 = sb("y2", (128, TB), F32, st2)
                            wtc = [sb("wtc%d" % i, (128, 3, KC, 128), BF16, st2) for i in range(2)]
                            gcs = sb("gcs", (128, TB), F32, st2)

                            def emit_bu(pr):
                                k8 = pr // 4
                                zb = pr % 2
                                for pl, Z in ((0, Zr2[zb]), (1, Zi2[zb])):
                                    for tb in range(NSB):
                                        S.op("pe", lambda p, pl=pl, tb=tb: p.matmul(
                                            psB[0][:], lhsB[:, pr, pl, :], u[:, k8, tb * TB:(tb + 1) * TB],
                                            start=True, stop=True),
                                            reads=["lhsB", "u%d" % k8], writes=["psB0"])
                                        S.op("act", lambda a, Z=Z, tb=tb: a.activation(
                                            out=Z[:, tb * TB:(tb + 1) * TB], in_=psB[0][:], func=AF.Copy),
                                            reads=["psB0"], writes=["Z%d" % zb])

                            def conv_proj(ci):
                                tb, j = ci // 8, ci % 8
                                b = ci % 2
                                tsl = slice(tb * TB, (tb + 1) * TB)
                                for gi in range(3):
                                    c0 = 1024 * (gi + 1) + j * 128
                                    S.dma("pool", wtc[b][:, gi], w_in_v[:, :, c0:c0 + 128], writes=["wtc%d" % b])

                                def proj(gi, pt, ptk):
                                    for kc in range(KC):
                                        S.op("pe", lambda p, kc=kc: p.matmul(
                                            pt[:], wtc[b][:, gi, kc, :], xT[:, kc, tsl],
                                            start=(kc == 0), stop=(kc == KC - 1)),
                                            reads=["wtc%d" % b, "xT%d" % tb], writes=[ptk], mark=(kc == KC - 1))
                                proj(1, psB[1], "psB1")
                                S.op("act", lambda a: a.activation(out=gcs[:], in_=psB[1][:], func=AF.Copy),
                                     reads=["psB1"], writes=["gcs"])
                                proj(2, psS, "psS")
                                proj(0, psB[1], "psB1")
                                S.op("act", lambda a: a.activation(out=gbs[:, j, tsl], in_=psB[1][:], func=AF.Copy),
                                     reads=["psB1"], writes=["gbs"])

                            def conv_mul(ci):
                                tb, j = ci // 8, ci % 8
                                S.op("dve", lambda v: v.tensor_tensor(chh[:, j, 2 + tb * TB:2 + (tb + 1) * TB], gcs[:], psS[:], ALU.mult),
                                     reads=["gcs", "psS"], writes=["chh"])

                            def scan_pair(pr):
                                k8, r = pr // 4, pr % 4
                                zb = pr % 2
                                Zr, Zi, sbf = Zr2[zb], Zi2[zb], sbf2[zb]
                                zk = "Z%d" % zb
                                sk = "sbf%d" % zb

                                def upd(k, dsl, ssl, last=False):
                                    ar = coef[:, pr, k, 0:1]; ai = coef[:, pr, k, 1:2]; nai = coef[:, pr, k, 2:3]
                                    ops = ((Zr, Zr, ar), (Zi, Zi, ar), (Zr, Zi, nai), (Zi, Zr, ai))
                                    for n_, (dz, sz, cc) in enumerate(ops):
                                        S.op("dve", lambda v, dz=dz, sz=sz, cc=cc: v.scalar_tensor_tensor(
                                            dz[:, dsl], sz[:, ssl], cc, dz[:, dsl], ALU.mult, ALU.add),
                                            reads=[zk, "coef"], writes=[zk], nosame=True,
                                            mark=(last and n_ == 3))
                                for k in range(NLEV):
                                    st_ = 2 ** (k + 1)
                                    upd(k, slice(st_ - 1, TS, st_), slice(2 ** k - 1, TS, st_))
                                for k in range(NLEV - 2, -1, -1):
                                    st_ = 2 ** (k + 1)
                                    upd(k, slice(st_ + 2 ** k - 1, TS, st_), slice(st_ - 1, TS - 2 ** k, st_), last=(k == 0))
                                S.op("act", lambda a: a.activation(out=sbf[:, 0, :], in_=Zr[:], func=AF.Copy),
                                     reads=[zk], writes=[sk])
                                S.op("act", lambda a: a.activation(out=sbf[:, 1, :], in_=Zi[:], func=AF.Copy),
                                     reads=[zk], writes=[sk])
                                for tb in range(NSB):
                                    for pl in range(2):
                                        S.op("pe", lambda p, tb=tb, pl=pl: p.matmul(
                                            psA[tb][:], lhsC[:, pr, pl, :], sbf[:, pl, tb * TB:(tb + 1) * TB],
                                            start=(r == 0 and pl == 0), stop=(r == 3 and pl == 1)),
                                            reads=["lhsC", sk], writes=["psA%d" % tb], mark=True)

                            def epilogue(k8):
                                for tb in range(NSB):
                                    tsl = slice(tb * TB, (tb + 1) * TB)
                                    S.op("dve", lambda v, tb=tb, tsl=tsl: v.scalar_tensor_tensor(
                                        yv[:], u[:, k8, tsl], dloc_sb[:, l, k8:k8 + 1], psA[tb][:], ALU.mult, ALU.add),
                                        reads=["u%d" % k8, "psA%d" % tb, "dloc"], writes=["yv"])
                                    S.op("dve", lambda v: v.tensor_tensor(y2[:], yv[:], yv[:], ALU.mult),
                                         reads=["yv"], writes=["y2"])
                                    S.op("dve", lambda v: v.tensor_scalar(y2[:], y2[:], 0.044715, 1.0, ALU.mult, ALU.add),
                                         reads=["y2"], writes=["y2"])
                                    S.op("dve", lambda v: v.tensor_tensor(y2[:], y2[:], yv[:], ALU.mult),
                                         reads=["y2", "yv"], writes=["y2"])
                                    S.op("act", lambda a: a.activation(out=y2[:], in_=y2[:], func=AF.Sigmoid,
                                                                       scale=2.0 * math.sqrt(2.0 / math.pi)),
                                         reads=["y2"], writes=["y2"])
                                    S.op("dve", lambda v, tsl=tsl: v.tensor_tensor(u[:, k8, tsl], y2[:], yv[:], ALU.mult),
                                         reads=["y2", "yv"], writes=["u%d" % k8])

                            nconv = NTB * 8
                            assert nconv <= NP
                            cidx = 0
                            emit_bu(0)
                            for pr in range(NP):
                                if pr + 1 < NP:
                                    emit_bu(pr + 1)
                                want = (nconv * (pr + 1) + NP - 1) // NP
                                todo = []
                                while cidx < want and not todo:
                                    conv_proj(cidx)
                                    todo.append(cidx)
                                    cidx += 1
                                scan_pair(pr)
                                for ci in todo:
                                    conv_mul(ci)
                                if pr % 4 == 3:
                                    epilogue(pr // 4)
                        S.barrier()
                        S.dma("sp", ch_in.ap().rearrange("p (a b) -> p a b", b=2), chh[:, :, T:T + 2],
                              reads=["chh"], writes=["ch_in"])
                        S.cc(lambda g: g.collective_compute("AllGather", ALU.bypass, replica_groups=groups,
                                                            ins=[ch_in.ap().opt()], outs=[ch_out.ap().opt()]),
                             reads=["ch_in"], writes=["ch_out"])
                        S.dma("sp", cy_in.ap().rearrange("(k p) t -> p k t", p=128), u[:], reads=["u"], writes=["cy_in"])
                        S.dma("sp", halo[:].rearrange("p a b -> p (a b)"), ch_out.ap()[0:128, :],
                              reads=["ch_out"], writes=["halo"])
                        S.op("dve", lambda v: v.tensor_scalar(chh[:, :, 0:2], halo[:], fl, None, ALU.mult),
                             reads=["halo", "flag", "chh"], writes=["chh"])
                        S.barrier()
                        S.cc(lambda g: g.collective_compute("AllGather", ALU.bypass, replica_groups=groups,
                                                            ins=[cy_in.ap().opt()], outs=[cy_out.ap().opt()]),
                             reads=[], writes=["cy_out"])
                    with ExitStack() as st2:
                        wglu2 = [sb("wglu%d" % i, (128, 8, 128), BF16, st2) for i in range(2)]
                        wgluv = w_glu[l].rearrange("(kc p) f -> p kc f", p=128)
                        y3 = sb("y3", (128, 16, TB), BF16, st2)
                        sq = [sb("sq%d" % i, (128, TB), BF16, st2) for i in range(2)]
                        wo = [sb("wo%d" % i, (128, KC, 256), BF16, st2) for i in range(2)]
                        xres = sb("xresm", (128, 4, D), F32, st2)
                        cv = sb("cv", (128, TB), F32, st2)
                        sgm = sb("sgm", (128, TB), F32, st2)
                        yf = sb("yf", (128, TB), F32, st2)
                        rst = sb("rst", (128, 2, 4), F32, st2)
                        rsacc = sb("rsacc", (128, 2, 4), F32, st2)
                        xbm = sb("xbm", (128, D), BF16, st2)
                        wov = w_out[l].rearrange("(kc p) d -> p kc d", p=128)
                        yg = sb("yg", (128, 8, T), BF16, st2)
                        ytmp = sb("ytmp", (128, 4, T), BF16, st2)
                        def load_yg():
                            cyv = cy_out.ap().rearrange("(k p) t -> p k t", p=128)
                            S.dma("sp", yg[:], cyv[:, :, 0:T], reads=["cy_out"], writes=["yg"])
                            S.op("dve", lambda v: v.tensor_scalar(yg[:], yg[:], nfl, None, ALU.mult),
                                 reads=["yg", "flag"], writes=["yg"])
                            for hf in range(2):
                                S.dma("sp", ytmp[:], cyv[:, hf * 4:(hf + 1) * 4, T:TS], reads=["cy_out"], writes=["ytmp"])
                                S.op("dve", lambda v, hf=hf: v.scalar_tensor_tensor(
                                    yg[:, hf * 4:(hf + 1) * 4, :], ytmp[:], fl, yg[:, hf * 4:(hf + 1) * 4, :], ALU.mult, ALU.add),
                                    reads=["yg", "ytmp", "flag"], writes=["yg"])

                        def stat_acc(j, half):
                            for tt in range(4):
                                S.op("pe", lambda p, tt=tt: p.matmul(
                                    psS[:, half * 4 + tt:half * 4 + tt + 1], sq[j % 2][:, tt * 128:(tt + 1) * 128], onesb[:],
                                    start=True, stop=True),
                                    reads=["sq%d" % (j % 2), "onesb"], writes=["psS"], mark=(tt == 3))
                            if j == 0:
                                S.op("dve", lambda v: v.tensor_copy(out=rsacc[:, half, :], in_=psS[:, half * 4:half * 4 + 4]),
                                     reads=["psS"], writes=["rsacc"])
                            else:
                                S.op("dve", lambda v: v.tensor_tensor(rsacc[:, half, :], rsacc[:, half, :],
                                                                      psS[:, half * 4:half * 4 + 4], ALU.add),
                                     reads=["psS", "rsacc"], writes=["rsacc"])

                        for tb in range(NTB):
                            tsl = slice(tb * TB, (tb + 1) * TB)
                            for tt in range(4):
                                g = tb * 4 + tt
                                S.dma("sp", xres[:, tt, :], xr[g * 128:(g + 1) * 128, :],
                                      reads=["xr%d" % g], writes=["xres%d" % tt])
                            o = tb * TB
                            for j in range(8):
                                S.op("dve", lambda v, j=j: v.tensor_scalar(cv[:], chh[:, j, o:o + TB], cp(1, j), cp(4, j),
                                                                          ALU.mult, ALU.add),
                                     reads=["chh", "chp"], writes=["cv"])
                                S.op("dve", lambda v, j=j: v.scalar_tensor_tensor(cv[:], chh[:, j, o + 1:o + 1 + TB], cp(2, j), cv[:],
                                                                                 ALU.mult, ALU.add),
                                     reads=["chh", "cv", "chp"], writes=["cv"])
                                S.op("dve", lambda v, j=j: v.scalar_tensor_tensor(cv[:], chh[:, j, o + 2:o + 2 + TB], cp(3, j), cv[:],
                                                                                 ALU.mult, ALU.add),
                                     reads=["chh", "cv", "chp"], writes=["cv"])
                                S.op("dve", lambda v, j=j: v.tensor_tensor(cv[:], cv[:], gbs[:, j, tsl], ALU.mult),
                                     reads=["cv", "gbs"], writes=["cv"])
                                S.op("act", lambda a, j=j: a.activation(out=sq[j % 2][:], in_=cv[:], func=AF.Square),
                                     reads=["cv"], writes=["sq%d" % (j % 2)])
                                stat_acc(j, 1)
                                S.op("dve", lambda v, j=j: v.tensor_scalar(y3[:, 8 + j, :], cv[:], cp(6, j), None, ALU.mult),
                                     reads=["cv", "chp"], writes=["y3"])
                            if tb == 0:
                                load_yg()
                            for j in range(8):
                                wglu = wglu2[j % 2]
                                S.dma("pool", wglu[:], wgluv[:, :, j * 128:(j + 1) * 128], writes=["wglu%d" % (j % 2)])
                                for kc in range(8):
                                    S.op("pe", lambda p, j=j, kc=kc, wglu=wglu: p.matmul(
                                        psA[2 + j % 2][:], wglu[:, kc, :], yg[:, kc, tsl],
                                        start=(kc == 0), stop=(kc == 7)),
                                        reads=["wglu%d" % (j % 2), "yg"], writes=["psA%d" % (2 + j % 2)], mark=(kc == 7))
                                if j > 0:
                                    stat_acc(j - 1, 0)
                                S.op("act", lambda a, j=j: a.activation(out=sgm[:], in_=psA[2 + j % 2][:], func=AF.Sigmoid),
                                     reads=["psA%d" % (2 + j % 2)], writes=["sgm"])
                                S.op("dve", lambda v, j=j: v.tensor_tensor(yf[:], sgm[:], yg[:, j, tsl], ALU.mult),
                                     reads=["sgm", "yg"], writes=["yf"])
                                S.op("act", lambda a, j=j: a.activation(out=sq[j % 2][:], in_=yf[:], func=AF.Square),
                                     reads=["yf"], writes=["sq%d" % (j % 2)])
                                S.op("dve", lambda v, j=j: v.tensor_scalar(y3[:, j, :], yf[:], cp(5, j), None, ALU.mult),
                                     reads=["yf", "chp"], writes=["y3"])
                            stat_acc(7, 0)
                            S.op("dve", lambda v: v.tensor_scalar(rst[:], rsacc[:], 1.0 / 1024.0, RMS_EPS, ALU.mult, ALU.add),
                                 reads=["rsacc"], writes=["rst"])
                            S.op("act", lambda a: a.activation(out=rst[:], in_=rst[:], func=AF.Sqrt),
                                 reads=["rst"], writes=["rst"])
                            S.op("dve", lambda v: v.reciprocal(rst[:], rst[:]), reads=["rst"], writes=["rst"])
                            S.op("dve", lambda v: v.tensor_scalar(rst[:], rst[:], 1.0 / ALPHA, None, ALU.mult),
                                 reads=["rst"], writes=["rst"])
                            stg = (sgm, yf)
                            stk = ("sgm", "yf")
                            for dt in range(D // 128):
                                wb = (dt // 2) % 2
                                sub = dt % 2
                                if sub == 0:
                                    c0 = (dt // 2) * 256
                                    S.dma("pool", wo[wb][:], wov[:, :, c0:c0 + 256], writes=["wo%d" % wb])
                                for half in range(2):
                                    for kc in range(8):
                                        S.op("pe", lambda p, half=half, kc=kc, wb=wb, sub=sub: p.matmul(
                                            psB[half][:], wo[wb][:, half * 8 + kc, sub * 128:(sub + 1) * 128],
                                            y3[:, half * 8 + kc, :], start=(kc == 0), stop=(kc == 7)),
                                            reads=["y3", "wo%d" % wb], writes=["psB%d" % half], mark=(kc == 7))
                                    S.op("act", lambda a, half=half: a.activation(out=stg[half][:], in_=psB[half][:], func=AF.Copy),
                                         reads=["psB%d" % half], writes=[stk[half]])
                                    for tt in range(4):
                                        S.op("pe", lambda p, tt=tt, half=half: p.transpose(
                                            psA[half][:, tt * 128:(tt + 1) * 128], stg[half][:, tt * 128:(tt + 1) * 128], identf[:]),
                                            reads=[stk[half], "identf"], writes=["psA%d" % half], mark=(tt == 3))
                                for tt in range(4):
                                    xs = xres[:, tt, dt * 128:(dt + 1) * 128]
                                    for half in range(2):
                                        S.op("dve", lambda v, tt=tt, xs=xs, half=half: v.scalar_tensor_tensor(
                                            xs, psA[half][:, tt * 128:(tt + 1) * 128], rst[:, half, tt:tt + 1], xs, ALU.mult, ALU.add),
                                            reads=["psA%d" % half, "rst", "xres%d" % tt], writes=["xres%d" % tt])
                            for tt in range(4):
                                layer_norm_tile(xres[:, tt, :], "xres%d" % tt, tb * 4 + tt,
                                                LN_EPS / (ALPHA * ALPHA), xr, xbm[:], ["xb"])
                S.barrier()

            for l in range(L):
                src = x_in if l == 0 else xr
                ffn_stage(l, w_f1g, w_f1u, w_f1d, 0, src, xr)
                if stop_after == "ffn1":
                    break
                mixer_stage(l)
                if stop_after == "mixer":
                    break
                ffn_stage(l, w_f2g, w_f2u, w_f2d, 2, xr, y_out if l == L - 1 else xr)
            if stop_after is not None:
                S.barrier()
                S.dma("sp", y_out, xr)
            S.barrier()
    return nc


_NC = None


def _prep(half, x, ffn1_gate, ffn1_up, ffn1_down, ln1_g, ln1_b, w_in, s5_lam_re, s5_lam_im, s5_log_dt,
          s5_b_re, s5_b_im, s5_c_re, s5_c_im, s5_d, s5_w_glu, conv_w, conv_b, g_s5, g_conv, w_out,
          ln2_g, ln2_b, ffn2_gate, ffn2_up, ffn2_down, ln3_g, ln3_b):
    f = lambda a: np.ascontiguousarray(np.asarray(a, dtype=np.float32))
    L = np.asarray(ffn1_gate).shape[0]
    lnp = np.stack([f(ln1_g), f(ln1_b), f(ln2_g), f(ln2_b), f(ln3_g), f(ln3_b)], axis=1)
    w_in = f(w_in)
    cw = f(conv_w)
    per_ch = np.stack([f(s5_d).reshape(L, 1024), cw[:, 0], cw[:, 1], cw[:, 2], f(conv_b), f(g_s5), f(g_conv)],
                      axis=1)
    chp = f(per_ch.reshape(L, 7, 8, 128).transpose(3, 0, 1, 2))
    shared = dict(ffn1_gate=f(ffn1_gate), ffn1_up=f(ffn1_up), ffn1_down=f(ffn1_down),
                  ffn2_gate=f(ffn2_gate), ffn2_up=f(ffn2_up), ffn2_down=f(ffn2_down),
                  w_in=w_in, s5_w_glu=f(s5_w_glu), w_out=f(w_out), lnp=lnp, chp=chp,
                  ident=np.eye(128, dtype=np.float32))
    lre_a = f(s5_lam_re).reshape(L, 32, 2, 64)
    lim_a = f(s5_lam_im).reshape(L, 32, 2, 64)
    ldt_a = np.broadcast_to(f(s5_log_dt).reshape(L, 32, 2, 1), (L, 32, 2, 64))
    b_re = f(s5_b_re).reshape(L, 32, 2, 64, 16); b_im = f(s5_b_im).reshape(L, 32, 2, 64, 16)
    c_re = f(s5_c_re).reshape(L, 32, 2, 16, 64); c_im = f(s5_c_im).reshape(L, 32, 2, 16, 64)
    dd = f(s5_d).reshape(L, 8, 128)
    per = []
    for h in range(2):
        ps_ = slice(16 * h, 16 * h + 16)
        s5s = np.stack([lre_a[:, ps_], lim_a[:, ps_], ldt_a[:, ps_]], axis=0)
        s5s = f(s5s.transpose(3, 4, 1, 0, 2).reshape(128, L, 3, 16))

        def pad(arr, tr):
            out = np.zeros((L, 128, 16, 128), np.float32)
            for pr in range(16):
                for gg in range(2):
                    g8 = (2 * pr + gg) % 8
                    blk = arr[:, 16 * h + pr, gg]
                    if tr:
                        blk = blk.transpose(0, 2, 1)
                    out[:, gg * 64:(gg + 1) * 64, pr, g8 * 16:(g8 + 1) * 16] = blk
            return out
        s5b = np.stack([pad(b_re, False), pad(b_im, False)], axis=1)
        s5c = np.stack([pad(c_re, True), pad(c_im, True)], axis=1)
        dloc = f(dd[:, 4 * h:4 * h + 4].transpose(2, 0, 1))
        wu = w_in[:, :, :1024]
        w_in_u = f(np.concatenate([wu[:, :, 512 * h:512 * h + 512], wu[:, :, 512 * (1 - h):512 * (1 - h) + 512]], axis=2))
        flag = np.zeros((128, 2), np.float32); flag[:, 0] = h; flag[:, 1] = 1 - h
        per.append(dict(s5s=s5s, s5b=s5b, s5c=s5c, dloc=dloc, w_in_u=w_in_u, flag=flag))
    return shared, per


def kernel(**inputs):
    global _NC
    x = np.ascontiguousarray(np.asarray(inputs["x"], dtype=np.float32))
    shared, per = _prep(None, **inputs)
    if _NC is None:
        _NC = build_nc(L=4, T=1024, FF=5632)
    in_maps = []
    for c in range(NCORES):
        b, h = c // 2, c % 2
        in_maps.append(dict(shared, x=np.ascontiguousarray(x[b, h * 1024:(h + 1) * 1024]), **per[h]))
    res = run_bass_kernel_spmd(_NC, in_maps, core_ids=list(range(NCORES)))
    out = np.empty((4, 2048, 2048), np.float32)
    for c in range(NCORES):
        b, h = c // 2, c % 2
        out[b, h * 1024:(h + 1) * 1024] = np.asarray(res.results[c]["y"], dtype=np.float32)
    return out
```
